# Optimizing a Trainium2 kernel written in Bass

```python
import math
import jax, jax.numpy as jnp
from jax import lax
import numpy as np

D_MODEL = 1024
BATCH = 16
SEQ = 2048
DEPTH = 2

DA_HEADS = 4
DA_HEAD_DIM = 64
DA_QK_WIDTH = DA_HEADS * 2 * DA_HEAD_DIM
DA_V_WIDTH = DA_HEADS * 2 * DA_HEAD_DIM
MLA_HEADS = 8
MLA_Q_LORA = 384
MLA_KV_LORA = 256
MLA_NOPE = 64
MLA_ROPE = 32
MLA_V = 64
ROPE_THETA = 10000.0
SC_WIDTH = 512
SC_KERNEL = 3
CF_WIDTH = 512
CF_KERNEL = 31
N_BRANCHES = 4
FFN_DIM = 2816
FFN_KERNEL = 3
N_MOD = 6
Q_BLOCK = 128
NORM_EPS = 1e-6
LN_EPS = 1e-5
IN_SIZES = (DA_QK_WIDTH, DA_QK_WIDTH, DA_V_WIDTH,
            MLA_Q_LORA, MLA_KV_LORA, MLA_ROPE,
            SC_WIDTH, SC_WIDTH, SC_WIDTH,
            2 * CF_WIDTH,
            N_BRANCHES * D_MODEL)
IN_WIDTH = (3 * DA_QK_WIDTH + MLA_Q_LORA + MLA_KV_LORA + MLA_ROPE
            + 3 * SC_WIDTH + 2 * CF_WIDTH + N_BRANCHES * D_MODEL)

kernel_name = "hybrid_gated_parallel_mixers"


def rmsnorm(x, g):
    xf = x.astype(jnp.float32)
    y = xf * lax.rsqrt(jnp.mean(xf * xf, axis=-1, keepdims=True) + NORM_EPS)
    return (y * g.astype(jnp.float32)).astype(x.dtype)


def layernorm(x, g, b):
    xf = x.astype(jnp.float32)
    mu = jnp.mean(xf, axis=-1, keepdims=True)
    var = jnp.mean(jnp.square(xf - mu), axis=-1, keepdims=True)
    y = (xf - mu) * lax.rsqrt(var + LN_EPS)
    return (y * g.astype(jnp.float32) + b.astype(jnp.float32)).astype(x.dtype)


def causal_dwconv(x, w):
    k, ch = w.shape
    return lax.conv_general_dilated(
        x, w[:, None, :].astype(x.dtype), window_strides=(1,), padding=[(k - 1, 0)],
        dimension_numbers=('NWC', 'WIO', 'NWC'), feature_group_count=ch)


def rope_tables(positions):
    half = MLA_ROPE // 2
    inv_freq = ROPE_THETA ** (-jnp.arange(half, dtype=jnp.float32) / half)
    ang = positions.astype(jnp.float32)[:, :, None, None] * inv_freq
    return jnp.cos(ang), jnp.sin(ang)


def apply_rope(t, cos, sin):
    t1, t2 = jnp.split(t.astype(jnp.float32), 2, axis=-1)
    return jnp.concatenate([t1 * cos - t2 * sin, t2 * cos + t1 * sin], axis=-1).astype(t.dtype)


def to_blocks(t):
    b, s = t.shape[:2]
    return jnp.moveaxis(t.reshape((b, s // Q_BLOCK, Q_BLOCK) + t.shape[2:]), 1, 0)


def from_blocks(t):
    t = jnp.moveaxis(t, 0, 1)
    return t.reshape((t.shape[0], -1) + t.shape[3:])


def block_starts(s):
    return jnp.arange(s // Q_BLOCK, dtype=jnp.int32) * Q_BLOCK


def causal_probs(q_blk, k, q_start, scale):
    s = jnp.einsum('bqhd,bkhd->bhqk', q_blk, k).astype(jnp.float32) * scale
    q_pos = q_start + jnp.arange(q_blk.shape[1], dtype=jnp.int32)
    k_pos = jnp.arange(k.shape[1], dtype=jnp.int32)
    mask = k_pos[None, :] <= q_pos[:, None]
    return jax.nn.softmax(jnp.where(mask, s, -jnp.inf), axis=-1)


def differential_attention(q1, q2, k1, k2, v, lam):
    scale = DA_HEAD_DIM ** -0.5

    def one_block(args):
        q1b, q2b, st = args
        p = causal_probs(q1b, k1, st, scale) - lam * causal_probs(q2b, k2, st, scale)
        return jnp.einsum('bhqk,bkhd->bqhd', p.astype(v.dtype), v)

    out = lax.map(one_block, (to_blocks(q1), to_blocks(q2), block_starts(q1.shape[1])))
    return from_blocks(out)


def softmax_attention(q, k, v):
    scale = (MLA_NOPE + MLA_ROPE) ** -0.5

    def one_block(args):
        qb, st = args
        p = causal_probs(qb, k, st, scale)
        return jnp.einsum('bhqk,bkhd->bqhd', p.astype(v.dtype), v)

    out = lax.map(one_block, (to_blocks(q), block_starts(q.shape[1])))
    return from_blocks(out)


def token_mixing(x, shift, scale, cos, sin, layer_idx, g_pre, w_in,
                 lq1, lk1, lq2, lk2, g_subln, da_w_o,
                 g_cq, w_uq, g_ckv, w_ukv, mla_w_o,
                 sc_conv, sc_w_o,
                 cf_conv, cf_conv_b, cf_ln_g, cf_ln_b, cf_w_o,
                 w_mix_out, g_post):
    bsz, s, _ = x.shape
    h = rmsnorm(x, g_pre) * (1.0 + scale) + shift
    proj = h @ w_in
    split_idx = []
    acc = 0
    for size in IN_SIZES[:-1]:
        acc += size
        split_idx.append(acc)
    (da_q, da_k, da_v, mla_cq, mla_ckv, mla_kr,
     sc_u, sc_b, sc_c, cf_in, gates) = jnp.split(proj, split_idx, axis=-1)

    da_q = da_q.reshape(bsz, s, DA_HEADS, 2, DA_HEAD_DIM)
    da_k = da_k.reshape(bsz, s, DA_HEADS, 2, DA_HEAD_DIM)
    da_v = da_v.reshape(bsz, s, DA_HEADS, 2 * DA_HEAD_DIM)
    lam_init = 0.8 - 0.6 * math.exp(-0.3 * layer_idx)
    lam = (jnp.exp(jnp.sum(lq1.astype(jnp.float32) * lk1.astype(jnp.float32)))
           - jnp.exp(jnp.sum(lq2.astype(jnp.float32) * lk2.astype(jnp.float32))) + lam_init)
    ya = differential_attention(da_q[:, :, :, 0], da_q[:, :, :, 1],
                                da_k[:, :, :, 0], da_k[:, :, :, 1], da_v, lam)
    ya = rmsnorm(ya, g_subln) * (1.0 - lam_init)
    ya = ya.reshape(bsz, s, DA_V_WIDTH) @ da_w_o

    q = (rmsnorm(mla_cq, g_cq) @ w_uq).reshape(bsz, s, MLA_HEADS, MLA_NOPE + MLA_ROPE)
    q_nope, q_rope = jnp.split(q, [MLA_NOPE], axis=-1)
    q = jnp.concatenate([q_nope, apply_rope(q_rope, cos, sin)], axis=-1)
    kv = (rmsnorm(mla_ckv, g_ckv) @ w_ukv).reshape(bsz, s, MLA_HEADS, MLA_NOPE + MLA_V)
    k_nope, v_b = jnp.split(kv, [MLA_NOPE], axis=-1)
    k_rope = apply_rope(mla_kr[:, :, None, :], cos, sin)
    k = jnp.concatenate([k_nope, jnp.broadcast_to(k_rope, (bsz, s, MLA_HEADS, MLA_ROPE))], axis=-1)
    yb = softmax_attention(q, k, v_b).reshape(bsz, s, MLA_HEADS * MLA_V) @ mla_w_o

    yc = (sc_b * causal_dwconv(sc_c * sc_u, sc_conv)) @ sc_w_o

    cf_a, cf_g = jnp.split(cf_in, 2, axis=-1)
    u = cf_a * jax.nn.sigmoid(cf_g)
    u = causal_dwconv(u, cf_conv) + cf_conv_b
    u = jax.nn.silu(layernorm(u, cf_ln_g, cf_ln_b))
    yd = u @ cf_w_o

    g = jax.nn.sigmoid(gates).reshape(bsz, s, N_BRANCHES, D_MODEL)
    merged = g[:, :, 0] * ya + g[:, :, 1] * yb + g[:, :, 2] * yc + g[:, :, 3] * yd
    return rmsnorm(merged @ w_mix_out, g_post)


def channel_mixing(x, shift, scale, g_pre, w_up, ffn_conv, w_down, g_post):
    h = rmsnorm(x, g_pre) * (1.0 + scale) + shift
    a, b = jnp.split(h @ w_up, 2, axis=-1)
    y = (jax.nn.silu(causal_dwconv(a, ffn_conv)) * b) @ w_down
    return rmsnorm(y, g_post)


def setup_inputs(seed: int = 0) -> dict:
    key = jax.random.key(seed)
    ks = iter(jax.random.split(key, 40))

    def nrm(shape, scale):
        return jax.random.normal(next(ks), shape, jnp.float32) * scale

    def gain(shape):
        return 1.0 + nrm(shape, 0.02)

    L = DEPTH
    return {
        "x": nrm((BATCH, SEQ, D_MODEL), 1.0),
        "c": nrm((BATCH, D_MODEL), 1.0),
        "positions": jnp.broadcast_to(jnp.arange(SEQ, dtype=jnp.int32), (BATCH, SEQ)),
        "w_ada": nrm((L, D_MODEL, N_MOD * D_MODEL), 0.5 * D_MODEL ** -0.5),
        "b_ada": nrm((L, N_MOD * D_MODEL), 0.01),
        "g_pre_mix": gain((L, D_MODEL)),
        "w_in": nrm((L, D_MODEL, IN_WIDTH), D_MODEL ** -0.5),
        "da_lam_q1": nrm((L, DA_HEAD_DIM), 0.1),
        "da_lam_k1": nrm((L, DA_HEAD_DIM), 0.1),
        "da_lam_q2": nrm((L, DA_HEAD_DIM), 0.1),
        "da_lam_k2": nrm((L, DA_HEAD_DIM), 0.1),
        "da_g_subln": gain((L, 2 * DA_HEAD_DIM)),
        "da_w_o": nrm((L, DA_V_WIDTH, D_MODEL), DA_V_WIDTH ** -0.5),
        "mla_g_cq": gain((L, MLA_Q_LORA)),
        "mla_w_uq": nrm((L, MLA_Q_LORA, MLA_HEADS * (MLA_NOPE + MLA_ROPE)), MLA_Q_LORA ** -0.5),
        "mla_g_ckv": gain((L, MLA_KV_LORA)),
        "mla_w_ukv": nrm((L, MLA_KV_LORA, MLA_HEADS * (MLA_NOPE + MLA_V)), MLA_KV_LORA ** -0.5),
        "mla_w_o": nrm((L, MLA_HEADS * MLA_V, D_MODEL), (MLA_HEADS * MLA_V) ** -0.5),
        "sc_conv": nrm((L, SC_KERNEL, SC_WIDTH), SC_KERNEL ** -0.5),
        "sc_w_o": nrm((L, SC_WIDTH, D_MODEL), SC_WIDTH ** -0.5),
        "cf_conv": nrm((L, CF_KERNEL, CF_WIDTH), CF_KERNEL ** -0.5),
        "cf_conv_b": nrm((L, CF_WIDTH), 0.01),
        "cf_ln_g": gain((L, CF_WIDTH)),
        "cf_ln_b": nrm((L, CF_WIDTH), 0.01),
        "cf_w_o": nrm((L, CF_WIDTH, D_MODEL), CF_WIDTH ** -0.5),
        "w_mix_out": nrm((L, D_MODEL, D_MODEL), D_MODEL ** -0.5),
        "g_post_mix": gain((L, D_MODEL)),
        "g_pre_ffn": gain((L, D_MODEL)),
        "w_up": nrm((L, D_MODEL, 2 * FFN_DIM), D_MODEL ** -0.5),
        "ffn_conv": nrm((L, FFN_KERNEL, FFN_DIM), FFN_KERNEL ** -0.5),
        "w_down": nrm((L, FFN_DIM, D_MODEL), FFN_DIM ** -0.5),
        "g_post_ffn": gain((L, D_MODEL)),
    }


def reference(x, c, positions, w_ada, b_ada, g_pre_mix, w_in,
              da_lam_q1, da_lam_k1, da_lam_q2, da_lam_k2, da_g_subln, da_w_o,
              mla_g_cq, mla_w_uq, mla_g_ckv, mla_w_ukv, mla_w_o,
              sc_conv, sc_w_o,
              cf_conv, cf_conv_b, cf_ln_g, cf_ln_b, cf_w_o,
              w_mix_out, g_post_mix, g_pre_ffn, w_up, ffn_conv, w_down, g_post_ffn):
    cos, sin = rope_tables(positions)
    c_act = jax.nn.silu(c)
    for l in range(DEPTH):
        mod = (c_act @ w_ada[l] + b_ada[l])[:, None, :]
        shift1, scale1, gate1, shift2, scale2, gate2 = jnp.split(mod, N_MOD, axis=-1)
        x = x + gate1 * token_mixing(
            x, shift1, scale1, cos, sin, l, g_pre_mix[l], w_in[l],
            da_lam_q1[l], da_lam_k1[l], da_lam_q2[l], da_lam_k2[l], da_g_subln[l], da_w_o[l],
            mla_g_cq[l], mla_w_uq[l], mla_g_ckv[l], mla_w_ukv[l], mla_w_o[l],
            sc_conv[l], sc_w_o[l],
            cf_conv[l], cf_conv_b[l], cf_ln_g[l], cf_ln_b[l], cf_w_o[l],
            w_mix_out[l], g_post_mix[l])
        x = x + gate2 * channel_mixing(
            x, shift2, scale2, g_pre_ffn[l], w_up[l], ffn_conv[l], w_down[l], g_post_ffn[l])
    return x
```

```python
import math
import numpy as np
import concourse.bass as bass
import concourse.mybir as mybir
from concourse.bass_utils import run_bass_kernel_spmd

F32 = mybir.dt.float32
BF16 = mybir.dt.bfloat16
I32 = mybir.dt.int32
AF = mybir.ActivationFunctionType
ALU = mybir.AluOpType
AX = mybir.AxisListType

D = 1024
KC = 8
TB = 512
IN_W = 8864
FFN = 2816
FC = 22
NORM_EPS = 1e-6
LN_EPS = 1e-5
OFF_DAQ, OFF_DAK, OFF_DAV, OFF_CQ, OFF_CKV, OFF_KR = 0, 512, 1024, 1536, 1920, 2176
OFF_SCU, OFF_SCB, OFF_SCC, OFF_CFA, OFF_CFG, OFF_GATE = 2208, 2720, 3232, 3744, 4256, 4768
SLOT_E = 3072
NFP = 10
NPOOLC = 5
WNAMES = ["w_ada", "b_ada", "g_pre_mix", "w_in", "da_lam_q1", "da_lam_k1", "da_lam_q2", "da_lam_k2",
          "da_g_subln", "da_w_o", "mla_g_cq", "mla_w_uq", "mla_g_ckv", "mla_w_ukv", "mla_w_o",
          "sc_conv", "sc_w_o", "cf_conv", "cf_conv_b", "cf_ln_g", "cf_ln_b", "cf_w_o",
          "w_mix_out", "g_post_mix", "g_pre_ffn", "w_up", "ffn_conv", "w_down", "g_post_ffn"]
WSHAPES = {
    "w_ada": [2, 1024, 6144], "b_ada": [2, 6144], "g_pre_mix": [2, 1024], "w_in": [2, 1024, IN_W],
    "da_lam_q1": [2, 64], "da_lam_k1": [2, 64], "da_lam_q2": [2, 64], "da_lam_k2": [2, 64],
    "da_g_subln": [2, 128], "da_w_o": [2, 512, 1024], "mla_g_cq": [2, 384], "mla_w_uq": [2, 384, 768],
    "mla_g_ckv": [2, 256], "mla_w_ukv": [2, 256, 1024], "mla_w_o": [2, 512, 1024],
    "sc_conv": [2, 3, 512], "sc_w_o": [2, 512, 1024], "cf_conv": [2, 31, 512], "cf_conv_b": [2, 512],
    "cf_ln_g": [2, 512], "cf_ln_b": [2, 512], "cf_w_o": [2, 512, 1024], "w_mix_out": [2, 1024, 1024],
    "g_post_mix": [2, 1024], "g_pre_ffn": [2, 1024], "w_up": [2, 1024, 2 * FFN], "ffn_conv": [2, 3, FFN],
    "w_down": [2, FFN, 1024], "g_post_ffn": [2, 1024],
}


class Sem:
    def __init__(self, h, key):
        self.h = h
        self.key = key
        self.count = 0


class Ev:
    __slots__ = ("sem", "val", "eng")

    def __init__(self, sem, val, eng):
        self.sem = sem
        self.val = val
        self.eng = eng


class Tile:
    __slots__ = ("w", "r", "excl")

    def __init__(self):
        self.w = []
        self.r = {}
        self.excl = False


class Eng:
    def __init__(self, name, h, sem):
        self.name = name
        self.h = h
        self.sem = sem
        self.count = 0
        self.waited = {}


class Buf:
    def __init__(self, h):
        self.h = h
        self.tiles = {}

    def t(self, key=0):
        tl = self.tiles.get(key)
        if tl is None:
            tl = Tile()
            self.tiles[key] = tl
        return tl


class Ring:
    def __init__(self, bufs):
        self.bufs = bufs
        self.i = 0

    def get(self):
        b = self.bufs[self.i % len(self.bufs)]
        self.i += 1
        return b


class Pool:
    def __init__(self, bufs):
        self.fl = list(bufs)
        self.n = len(bufs)

    def get(self):
        assert self.fl, "pool exhausted"
        return self.fl.pop(0)

    def free(self, *bs):
        for b in bs:
            assert b not in self.fl
            self.fl.append(b)


class KB:
    def __init__(self, S, NSEQ, NL=2, NSLOT=3, NDS=24):
        self.S, self.NSEQ, self.NL, self.NSLOT = S, NSEQ, NL, NSLOT
        self.NB = S // TB
        nc = bass.Bass("TRN2", target_bir_lowering=False)
        self.nc = nc
        nsem = [0]

        def mksem(name):
            s = Sem(nc.alloc_semaphore(name), nsem[0])
            nsem[0] += 1
            return s
        self.eng = {
            "pe": Eng("pe", nc.tensor, mksem("s_pe")),
            "act": Eng("act", nc.scalar, mksem("s_act")),
            "dve": Eng("dve", nc.vector, mksem("s_dve")),
            "pool": Eng("pool", nc.gpsimd, mksem("s_pool")),
            "sp": Eng("sp", nc.sync, mksem("s_sp")),
        }
        self.dsems = {"sp": [mksem(f"s_dma{i}") for i in range(NDS)], "pool": [mksem(f"s_pdma{i}") for i in range(NDS)],
                      "act": [mksem(f"s_adma{i}") for i in range(8)]}
        self.dsi = {"sp": 0, "pool": 0, "act": 0}
        self.out_evs = []

    def _wait(self, e, evs):
        need = {}
        for ev in evs:
            k = ev.sem.key
            if k not in need or need[k].val < ev.val:
                need[k] = ev
        for k, ev in need.items():
            if e.waited.get(k, 0) < ev.val:
                e.h.wait_ge(ev.sem.h, ev.val)
                e.waited[k] = ev.val

    def _collect(self, e, R, W, PW):
        evs = []
        for t in R:
            evs.extend(t.w)
            if t.excl:
                for ev in t.r.values():
                    if ev.eng != e.name:
                        evs.append(ev)
        skip = "pe" if e.name == "pe" else None
        for t in W:
            for ev in t.w:
                if ev.eng != skip:
                    evs.append(ev)
            for ev in t.r.values():
                if ev.eng != skip:
                    evs.append(ev)
        for t in PW:
            for ev in t.r.values():
                if ev.eng != skip:
                    evs.append(ev)
        return evs

    def _record(self, ev, R, W, PW):
        rk = ("d", ev.sem.key) if ev.eng == "dma" else ev.eng
        for t in R:
            t.r[rk] = ev
        for t in W:
            t.w = [ev]
            t.r = {}
        for t in PW:
            t.w.append(ev)

    def E(self, en, fn, R=(), W=(), PW=(), sig=True):
        e = self.eng[en]
        self._wait(e, self._collect(e, R, W, PW))
        ins = fn(e.h)
        if sig:
            e.count += 1
            ins.then_inc(e.sem.h, 1)
            ev = Ev(e.sem, e.count, en)
        else:
            ev = Ev(e.sem, e.count + 1, en)
        self._record(ev, R, W, PW)
        return ev

    def dma(self, q, out, in_, R=(), W=(), PW=()):
        e = self.eng[q]
        self._wait(e, self._collect(e, R, W, PW))
        sem = self.dsems[q][self.dsi[q] % len(self.dsems[q])]
        self.dsi[q] += 1
        if e.waited.get(sem.key, 0) < sem.count:
            e.h.wait_ge(sem.h, sem.count)
            e.waited[sem.key] = sem.count
        ins = e.h.dma_start(out=out, in_=in_)
        ins.then_inc(sem.h, 16)
        sem.count += 16
        ev = Ev(sem, sem.count, "dma")
        self._record(ev, R, W, PW)
        return ev

    def mm(self, out, lhsT, rhs, start, stop, R, W, sig=None):
        sig = True
        return self.E("pe", lambda h: h.matmul(out, lhsT, rhs, start=start, stop=stop), R=R, W=W, sig=sig)

    def tr(self, out, in_, ident, R, W):
        return self.E("pe", lambda h: h.transpose(out, in_, ident), R=R, W=W, sig=True)

    def act(self, out, in_, func, R, W, PW=(), **kw):
        return self.E("act", lambda h: h.activation(out=out, in_=in_, func=func, **kw), R=R, W=W, PW=PW)

    def V(self, en, meth, R, W, PW=(), **kw):
        return self.E(en, lambda h: getattr(h, meth)(**kw), R=R, W=W, PW=PW)

    def build(self):
        nc, S, NSEQ, NL, NB = self.nc, self.S, self.NSEQ, self.NL, self.NB
        dr = {}
        dr["x"] = nc.dram_tensor("x", [NSEQ, S, D], F32, kind="ExternalInput").ap()
        dr["c"] = nc.dram_tensor("c", [NSEQ, D], F32, kind="ExternalInput").ap()
        dr["positions"] = nc.dram_tensor("positions", [NSEQ, S], I32, kind="ExternalInput").ap()
        for n in WNAMES:
            dr[n] = nc.dram_tensor(n, WSHAPES[n], F32, kind="ExternalInput").ap()
        self.dr = dr
        self.out = nc.dram_tensor("out", [NSEQ, S, D], F32, kind="ExternalOutput").ap()

        names = ["va", "vb", "qk0a", "qk0b", "qk1a", "qk1b", "c", "c2", "uq", "ukv",
                 "sc0", "sc1", "sc2", "sc3"]
        names += [f"cf{j}" for j in range(4)]
        for j in range(4):
            names += [f"cda{j}", f"cdb{j}"]
        for j in range(8):
            names += [f"ma{j}", f"mb{j}"]
        names += [f"mix{t}" for t in range(4)] + [f"up{t}" for t in range(FC)] + [f"dn{j}" for j in range(8)]
        self.tnames = names
        self.tidx = {n: i for i, n in enumerate(names)}

        def tsz(n):
            if n in ("c", "uq") or n[:2] in ("sc", "ma", "mb"):
                return 3072
            if n == "c2":
                return 2560
            if n[:2] == "dn":
                return FC * 128
            if n[:3] == "cdb":
                return 15 * 128
            return 2048
        self.tsize = [tsz(n) for n in names]
        NT = len(names)
        self.scr = [nc.dram_tensor(f"scr{l}", [NT, 128, SLOT_E], BF16, kind="Internal").ap() for l in range(NL)]
        self.scr_t = [[Tile() for _ in range(NT)] for _ in range(NL)]

        A = nc.alloc_sbuf_tensor
        self.xT = Buf(A("xT", [128, KC, S], F32))
        self.KTda = Buf(A("KTda", [128, 4, S], BF16))
        self.Vda = Buf(A("Vda", [128, S // 128, 512], BF16))
        self.lat = Buf(A("lat", [128, 2, S], BF16))
        self.KTw = [Buf(A(f"KTw{i}", [128, S], BF16)) for i in range(2)]
        self.VTw = [Buf(A(f"VTw{i}", [128, S // 128, 128], BF16)) for i in range(2)]
        self.slots = [Buf(A(f"slot{i}", [128, SLOT_E], BF16)) for i in range(self.NSLOT)]
        self.bfp = Pool([Buf(A(f"bf{i}", [128, 512], BF16)) for i in range(6)])
        self.pring = Ring([Buf(A(f"pt{i}", [128, 512], BF16)) for i in range(4)])
        self.hT = Buf(A("hT", [128, KC, 512], BF16))
        self.big = Buf(A("big", [128, 24, 512], BF16))
        self.fp = Pool([Buf(A(f"f{i}", [128, 544], F32)) for i in range(NFP)])
        self.ident = Buf(A("ident", [128, 128], F32))
        self.ones = Buf(A("ones", [128, 128], BF16))
        self.tri = Buf(A("tri", [128, 128], BF16))
        self.identb = Buf(A("identb", [128, 128], BF16))
        self.onerow = Buf(A("onerow", [1, 128], F32))
        self.invf = Buf(A("invf", [128, 1], F32))
        self.PA = [Buf(A(f"PA{l}", [128, 110], F32)) for l in range(NL)]
        self.PB = [Buf(A(f"PB{l}", [128, 124], F32)) for l in range(NL)]
        self.PC = [Buf(A(f"PC{l}", [128, 66], F32)) for l in range(NL)]
        self.modT = [Buf(A(f"modT{l}", [128, 48, NSEQ], F32)) for l in range(NL)]
        self.cT = Buf(A("cT", [128, NSEQ * 8], F32))
        self.sc1 = Buf(A("sc1", [128, 4, 8], F32))
        self.lamneg = [Buf(A(f"lamneg{l}", [128, 1], F32)) for l in range(NL)]
        self.gsub = [Buf(A(f"gsub{l}", [128, 1], F32)) for l in range(NL)]
        self.gneg = Buf(A("gneg", [128, 4], F32))
        self.halo_sc = Buf(A("halo_sc", [128, 4, 2], F32))
        self.halo_cf = Buf(A("halo_cf", [128, 4, 30], BF16))
        self.halo_ff = Buf(A("halo_ff", [128, FC, 2], F32))
        self.small = Buf(A("small", [128, 8], F32))
        self.ps = [Buf(nc.alloc_psum_tensor(f"ps{i}", [128, 512], F32)) for i in range(8)]
        for pb in self.ps:
            pb.t().excl = True
        self.ps_tiles = {}
        self.psp = Pool(list(self.ps))
        self.psi = 0
        self.sri = 0
        self.ori = 0
        self.ps_last = [0] * 8
        self.psclk = 0

        import os as _os
        STOP = int(_os.environ.get("KSTOP", "99"))
        self.STOP = STOP
        self.consts()
        if STOP <= 1:
            return nc
        for l in range(NL):
            self.small_params(l)
        if STOP <= 2:
            return nc
        self.deferred = []
        self.prep0 = {}
        self.prep0_next = 0
        for l in range(NL):
            self.prep_layer(l)
        self.ensure_prep0(24)
        self.load_x_block(0, 0)
        self.mod_all()
        self.diag_i = 0
        for l in range(NL):
            self.prep_diag(l)
        if STOP <= 4:
            return nc
        self.deferred_all = self.deferred
        self.deferred_done = 0
        if STOP <= 4:
            return nc

        self.sched = []
        for s in range(NSEQ):
            for l in range(NL):
                for b in range(NB):
                    for n in names:
                        self.sched.append((l, self.tidx[n]))
        self.si = 0
        self.loaded = 0

        xloaded = {(0, 0)}
        for s in range(NSEQ):
            for l in range(NL):
                if l >= 1:
                    self.run_deferred(1.0)
                self.layer_seq_init(l, s)
                for b in range(NB):
                    if l == 0 and (s, b) not in xloaded:
                        self.load_x_block(s, b)
                        xloaded.add((s, b))
                    self.mixer_block(l, s, b, skip_norm=(b > 0))

                    if l == 0 and b + 1 < NB and (s, b + 1) not in xloaded:
                        self.load_x_block(s, b + 1)
                        xloaded.add((s, b + 1))
                    if l == NL - 1 and b == NB - 1 and s + 1 < NSEQ:
                        self.load_x_block(s + 1, 0)
                        xloaded.add((s + 1, 0))
                    self.ffn_block(l, s, b, hoist=(b + 1 < NB))
                    if l == NL - 1:
                        self.store_x_block(s, b)
        self._wait(self.eng["sp"], self.out_evs)
        return nc

    def psn(self):
        return self.psp.get()

    def psn4(self):
        return self.psp.get()

    def psf(self, *bs):
        self.psp.free(*bs)

    def consts(self):
        f = self.fp.get()
        ii = f.h[:, 0:128].bitcast(I32)
        self.E("pool", lambda h: h.iota(ii, [[1, 128]], base=0, channel_multiplier=-1), W=[f.t()])
        f2 = self.fp.get()
        self.V("dve", "tensor_copy", R=[f.t()], W=[f2.t()], out=f2.h[:, 0:128], in_=ii)
        self.V("dve", "tensor_single_scalar", R=[f2.t()], W=[self.ident.t()], out=self.ident.h[:], in_=f2.h[:, 0:128], scalar=0.0, op=ALU.is_equal)
        self.V("dve", "tensor_scalar", R=[f2.t()], W=[self.tri.t()], out=self.tri.h[:], in0=f2.h[:, 0:128], scalar1=0.0, scalar2=-30000.0, op0=ALU.is_lt, op1=ALU.mult)
        self.V("dve", "tensor_single_scalar", R=[f2.t()], W=[self.identb.t()], out=self.identb.h[:], in_=f2.h[:, 0:128], scalar=0.0, op=ALU.is_equal)
        self.fp.free(f, f2)
        self.V("pool", "memset", R=[], W=[self.ones.t()], ap=self.ones.h[:], constant=1.0)
        self.V("pool", "memset", R=[], W=[self.onerow.t()], ap=self.onerow.h[:], constant=1.0)
        self.V("pool", "memset", R=[], W=[self.small.t()], ap=self.small.h[:, 0:1], constant=float(NORM_EPS))
        self.V("pool", "memset", R=[], W=[], PW=[self.small.t()], ap=self.small.h[:, 1:2], constant=float(LN_EPS))
        fa = self.fp.get()
        pidx = fa.h[:, 0:1].bitcast(I32)
        self.E("pool", lambda h: h.iota(pidx, [[0, 1]], base=0, channel_multiplier=1), W=[fa.t()])
        fb = self.fp.get()
        self.V("dve", "tensor_copy", R=[fa.t()], W=[fb.t()], out=fb.h[:, 0:1], in_=pidx)
        self.V("dve", "tensor_single_scalar", R=[fb.t()], W=[], PW=[fb.t()], out=fb.h[:, 1:2], in_=fb.h[:, 0:1], scalar=96.0, op=ALU.is_ge)
        self.V("dve", "tensor_single_scalar", R=[fb.t()], W=[], PW=[fb.t()], out=fb.h[:, 2:3], in_=fb.h[:, 0:1], scalar=112.0, op=ALU.is_lt)
        self.V("dve", "tensor_tensor", R=[fb.t()], W=[], PW=[fb.t()], out=fb.h[:, 3:4], in0=fb.h[:, 1:2], in1=fb.h[:, 2:3], op=ALU.mult)
        TWO_PI = float(2.0 * math.pi * (1.0 - 1e-6))
        self.V("dve", "tensor_scalar", R=[fb.t()], W=[], PW=[self.small.t()], out=self.small.h[:, 2:3], in0=fb.h[:, 3:4], scalar1=-2.0 * TWO_PI, scalar2=TWO_PI, op0=ALU.mult, op1=ALU.add)
        self.fp.free(fa, fb)
        f3 = self.fp.get()
        rowv = f3.h[0:1, 0:128].rearrange("o (r i) -> o r i", i=16)
        for i in range(16):
            val = (10000.0 ** (-i / 16.0)) / (2.0 * math.pi)
            self.V("pool", "memset", R=[], W=[f3.t()] if i == 0 else [], PW=[] if i == 0 else [f3.t()], ap=rowv[:, :, i], constant=float(val))
        p = self.psn()
        self.mm(p.h[:, 0:1], f3.h[0:1, 0:128], self.onerow.h[0:1, 0:1], True, True, R=[f3.t(), self.onerow.t()], W=[p.t()])
        self.V("dve", "tensor_copy", R=[p.t()], W=[self.invf.t()], out=self.invf.h[:], in_=p.h[:, 0:1])
        self.psf(p)
        self.fp.free(f3)
        for i in range(2):
            self.V("pool", "memset", R=[], W=[self.VTw[i].t(0), self.VTw[i].t(1)], ap=self.VTw[i].h[:, :, 64:128], constant=1.0)
            self.V("pool", "memset", R=[], W=[self.KTw[i].t("z")], ap=self.KTw[i].h[96:128, :], constant=0.0)
        self.V("pool", "memset", R=[], W=[self.big.t(c) for c in range(24)], ap=self.big.h[:], constant=0.0)

    def epsb(self, eps):
        return self.small.h[:, 0:1] if eps == NORM_EPS else self.small.h[:, 1:2]

    def loadT(self, dst_ap, dst_t, rows, parts):
        st = self.fp.get()
        first = True
        for ap, r0 in parts:
            n = ap.shape[0]
            self.dma("sp", st.h[r0:r0 + n, 0:128], ap, W=[st.t()] if first else [], PW=[] if first else [st.t()])
            first = False
        p = self.psn()
        self.tr(p.h[:, 0:rows], st.h[0:rows, 0:128], self.ident.h[0:rows, 0:rows], R=[st.t(), self.ident.t()], W=[p.t()])
        self.V("dve", "tensor_copy", R=[p.t()], W=[dst_t], out=dst_ap, in_=p.h[:, 0:rows])
        self.psf(p)
        self.fp.free(st)

    def small_params(self, l):
        dr = self.dr
        v8 = lambda n: dr[n][l].rearrange("(c p) -> c p", p=128)
        parts = [(v8("g_pre_mix"), 0), (v8("g_post_mix"), 8), (v8("g_pre_ffn"), 16), (v8("g_post_ffn"), 24),
                 (v8("b_ada"), 32), (v8("da_g_subln"), 80), (v8("mla_g_cq"), 81), (v8("mla_g_ckv"), 84),
                 (dr["sc_conv"][l].rearrange("k (c p) -> (k c) p", p=128), 86),
                 (v8("cf_conv_b"), 98), (v8("cf_ln_g"), 102), (v8("cf_ln_b"), 106)]
        self.loadT(self.PA[l].h[:], self.PA[l].t(), 110, parts)
        self.loadT(self.PB[l].h[:], self.PB[l].t(), 124, [(dr["cf_conv"][l].rearrange("k (c p) -> (k c) p", p=128), 0)])
        self.loadT(self.PC[l].h[:], self.PC[l].t(), 66, [(dr["ffn_conv"][l].rearrange("k (c p) -> (k c) p", p=128), 0)])
        lam_init = 0.8 - 0.6 * math.exp(-0.3 * l)
        self.V("dve", "tensor_scalar", R=[self.PA[l].t()], W=[self.gsub[l].t()], out=self.gsub[l].h[:], in0=self.PA[l].h[:, 80:81],
               scalar1=float(1.0 - lam_init), scalar2=None, op0=ALU.mult)
        st = self.fp.get()
        for i, n in enumerate(["da_lam_q1", "da_lam_k1", "da_lam_q2", "da_lam_k2"]):
            self.dma("sp", st.h[0:1, 64 * i:64 * i + 64], dr[n][l:l + 1, :], W=[st.t()] if i == 0 else [], PW=[] if i == 0 else [st.t()])
        t2, t3, t4, t5 = self.fp.get(), self.fp.get(), self.fp.get(), self.fp.get()
        self.V("dve", "tensor_tensor", R=[st.t()], W=[t2.t()], out=t2.h[0:1, 0:64], in0=st.h[0:1, 0:64], in1=st.h[0:1, 64:128], op=ALU.mult)
        self.V("dve", "tensor_tensor", R=[st.t()], W=[t3.t()], out=t3.h[0:1, 0:64], in0=st.h[0:1, 128:192], in1=st.h[0:1, 192:256], op=ALU.mult)
        self.V("dve", "reduce_sum", R=[t2.t()], W=[t4.t()], out=t4.h[0:1, 0:1], in_=t2.h[0:1, 0:64], axis=AX.X)
        self.V("dve", "reduce_sum", R=[t3.t()], W=[t5.t()], out=t5.h[0:1, 0:1], in_=t3.h[0:1, 0:64], axis=AX.X)
        t6, t7 = self.fp.get(), self.fp.get()
        self.act(t6.h[0:1, 0:1], t4.h[0:1, 0:1], AF.Exp, R=[t4.t()], W=[t6.t()])
        self.act(t7.h[0:1, 0:1], t5.h[0:1, 0:1], AF.Exp, R=[t5.t()], W=[t7.t()])
        self.V("dve", "tensor_tensor", R=[t6.t(), t7.t()], W=[t2.t()], out=t2.h[0:1, 0:1], in0=t6.h[0:1, 0:1], in1=t7.h[0:1, 0:1], op=ALU.subtract)
        p = self.psn()
        self.mm(p.h[:, 0:1], self.onerow.h[0:1, 0:128], t2.h[0:1, 0:1], True, True, R=[t2.t(), self.onerow.t()], W=[p.t()])
        self.V("dve", "tensor_scalar", R=[p.t()], W=[self.lamneg[l].t()], out=self.lamneg[l].h[:], in0=p.h[:, 0:1],
               scalar1=-1.0, scalar2=float(-lam_init), op0=ALU.mult, op1=ALU.add)
        self.psf(p)
        self.fp.free(st, t2, t3, t4, t5, t6, t7)

    def mod_all(self):
        dr, NSEQ = self.dr, self.NSEQ
        n8 = NSEQ * 8
        st = self.fp.get()
        self.dma("sp", st.h[0:n8, 0:128], dr["c"].rearrange("s (c p) -> (s c) p", p=128), W=[st.t()])
        p = self.psn()
        self.tr(p.h[:, 0:n8], st.h[0:n8, 0:128], self.ident.h[0:n8, 0:n8], R=[st.t(), self.ident.t()], W=[p.t()])
        self.act(self.cT.h[:], p.h[:, 0:n8], AF.Silu, R=[p.t()], W=[self.cT.t()])
        self.psf(p)
        self.fp.free(st)
        cv = self.cT.h[:].rearrange("p (s c) -> p s c", c=8)
        k = 0
        for l in range(self.NL):
            firstw = True
            for ct in range(12):
                p = self.psn()
                for gi, (k0, k1) in enumerate([(0, 3), (3, 6), (6, 8)]):
                    nk = k1 - k0
                    sl = self.slots[k % self.NSLOT]
                    k += 1
                    sf = sl.h[:].bitcast(F32)[:, 0:nk * 512].rearrange("p (k n) -> p k n", k=nk)
                    self.dma("sp", sf, dr["w_ada"][l][k0 * 128:k1 * 128, ct * 512:(ct + 1) * 512].rearrange("(k p) n -> p k n", p=128), W=[sl.t()])
                    for kc in range(nk):
                        self.mm(p.h[0:NSEQ, 0:512], cv[:, :, k0 + kc], sf[:, kc, :], k0 + kc == 0, k0 + kc == 7, R=[self.cT.t(), sl.t()], W=[p.t()])
                row = self.fp.get()
                self.V("dve", "tensor_copy", R=[p.t()], W=[row.t()], out=row.h[0:NSEQ, 0:512], in_=p.h[0:NSEQ, 0:512])
                self.psf(p)
                for q in range(4):
                    j = ct * 4 + q
                    p2 = self.psn()
                    self.tr(p2.h[:, 0:NSEQ], row.h[0:NSEQ, q * 128:(q + 1) * 128], self.ident.h[0:NSEQ, 0:NSEQ], R=[row.t(), self.ident.t()], W=[p2.t()])
                    self.V("dve", "tensor_scalar", R=[p2.t(), self.PA[l].t()], W=[self.modT[l].t()] if firstw else [], PW=[] if firstw else [self.modT[l].t()],
                           out=self.modT[l].h[:, j, :], in0=p2.h[:, 0:NSEQ], scalar1=self.PA[l].h[:, 32 + j:33 + j], scalar2=None, op0=ALU.add)
                    self.psf(p2)
                    firstw = False
                self.fp.free(row)

    def seg(self, l, tname, off, src, kc, n):
        ti = self.tidx[tname]
        dst = self.scr[l][ti][:, off:off + kc * n].rearrange("p (k n) -> p k n", k=kc)
        srcv = src.rearrange("(k p) n -> p k n", p=128)
        fn = lambda: self.dma("pool", dst, srcv, PW=[self.scr_t[l][ti]])
        self.reg_prep(l, ti, fn)

    def reg_prep(self, l, ti, fn):
        if l == 0:
            self.prep0.setdefault(ti, []).append(fn)
        else:
            self.deferred.append(fn)

    def ensure_prep0(self, upto):
        while self.prep0_next < min(upto, len(self.tnames)):
            for fn in self.prep0.pop(self.prep0_next, []):
                fn()
            self.prep0_next += 1

    def run_deferred(self, frac):
        n = int(math.ceil(len(self.deferred_all) * frac))
        while self.deferred_done < n:
            self.deferred_all[self.deferred_done]()
            self.deferred_done += 1

    def prep_layer(self, l):
        dr = self.dr
        win = dr["w_in"][l]
        self.seg(l, "qk0a", 0, win[:, OFF_DAQ:OFF_DAQ + 256], 8, 256)
        self.seg(l, "qk0b", 0, win[:, OFF_DAQ + 256:OFF_DAQ + 512], 8, 256)
        self.seg(l, "qk1a", 0, win[:, OFF_DAK:OFF_DAK + 256], 8, 256)
        self.seg(l, "qk1b", 0, win[:, OFF_DAK + 256:OFF_DAK + 512], 8, 256)
        self.seg(l, "va", 0, win[:, OFF_DAV:OFF_DAV + 256], 8, 256)
        self.seg(l, "vb", 0, win[:, OFF_DAV + 256:OFF_DAV + 512], 8, 256)
        self.seg(l, "c", 0, win[:, OFF_CQ:OFF_CQ + 384], 8, 384)
        ti = self.tidx["c2"]
        c2v = self.scr[l][ti][:, 0:8 * 320].rearrange("p (k n) -> p k n", k=8)
        fn_c2 = lambda c2v=c2v, ti=ti: self.dma("pool", c2v[:, :, 0:288], win[:, OFF_CKV:OFF_CKV + 288].rearrange("(k p) n -> p k n", p=128), PW=[self.scr_t[l][ti]])
        self.reg_prep(l, ti, fn_c2)
        self.reg_prep(l, ti, lambda c2v=c2v, ti=ti: self.dma("pool", c2v[:, :, 288:304], win[:, OFF_KR + 16:OFF_KR + 32].rearrange("(k p) n -> p k n", p=128), PW=[self.scr_t[l][ti]]))
        self.reg_prep(l, ti, lambda c2v=c2v, ti=ti: self.dma("pool", c2v[:, :, 304:320], win[:, OFF_KR:OFF_KR + 16].rearrange("(k p) n -> p k n", p=128), PW=[self.scr_t[l][ti]]))
        ti = self.tidx["uq"]
        uqv = self.scr[l][ti][:, 0:3072].rearrange("p (k h n) -> p k h n", k=3, h=8)
        for kc in range(3):
            srcv = dr["mla_w_uq"][l][kc * 128:(kc + 1) * 128, :].rearrange("p (h n) -> p h n", h=8)
            for (d0, d1, s0, s1) in [(0, 96, 0, 96), (96, 112, 80, 96), (112, 128, 64, 80)]:
                self.reg_prep(l, ti, lambda kc=kc, srcv=srcv, d0=d0, d1=d1, s0=s0, s1=s1, ti=ti, uqv=uqv:
                              self.dma("pool", uqv[:, kc, :, d0:d1], srcv[:, :, s0:s1], PW=[self.scr_t[l][ti]]))
        ti = self.tidx["ukv"]
        ukvv = self.scr[l][ti][:, 0:2048].rearrange("p (k t h n) -> p k t h n", k=2, t=2, h=8)
        for kc in range(2):
            srcv = dr["mla_w_ukv"][l][kc * 128:(kc + 1) * 128, :].rearrange("p (h t n) -> p h t n", h=8, t=2)
            for t in range(2):
                self.reg_prep(l, ti, lambda kc=kc, t=t, srcv=srcv, ti=ti, ukvv=ukvv:
                              self.dma("pool", ukvv[:, kc, t, :, :], srcv[:, :, t, :], PW=[self.scr_t[l][ti]]))
        for j in range(4):
            for q, off in enumerate([OFF_SCU, OFF_SCC, OFF_SCB]):
                self.seg(l, f"sc{j}", q * 1024, win[:, off + j * 128:off + (j + 1) * 128], 8, 128)
            self.seg(l, f"cf{j}", 0, win[:, OFF_CFA + j * 128:OFF_CFA + (j + 1) * 128], 8, 128)
            self.seg(l, f"cf{j}", 1024, win[:, OFF_CFG + j * 128:OFF_CFG + (j + 1) * 128], 8, 128)
        for j in range(8):
            for i in range(4):
                tn = f"ma{j}" if i < 2 else f"mb{j}"
                o = OFF_GATE + i * 1024 + j * 128
                self.seg(l, tn, (i % 2) * 1024, win[:, o:o + 128], 8, 128)
            for i, wn in [(0, "da_w_o"), (1, "mla_w_o"), (2, "sc_w_o"), (3, "cf_w_o")]:
                tn = f"ma{j}" if i < 2 else f"mb{j}"
                self.seg(l, tn, 2048 + (i % 2) * 512, dr[wn][l][:, j * 128:(j + 1) * 128], 4, 128)
        for t in range(4):
            self.seg(l, f"mix{t}", 0, dr["w_mix_out"][l][:, t * 256:(t + 1) * 256], 8, 256)
        wup = dr["w_up"][l]
        for c in range(FC):
            self.seg(l, f"up{c}", 0, wup[:, c * 128:(c + 1) * 128], 8, 128)
            self.seg(l, f"up{c}", 1024, wup[:, FFN + c * 128:FFN + (c + 1) * 128], 8, 128)
        for j in range(8):
            self.seg(l, f"dn{j}", 0, dr["w_down"][l][:, j * 128:(j + 1) * 128], FC, 128)

    def prep_diag(self, l):
        for j in range(4):
            for part, (k0, k1) in enumerate([(0, 16), (16, 31)]):
                cb = (self.diag_i % 6) * 4
                self.diag_i += 1
                tiles = [self.big.t(cb + q) for q in range(4)]
                st = self.big.h[:, cb:cb + 4, :].rearrange("p a b -> p (a b)")
                first = True
                for k in range(k0, k1):
                    self.V("dve", "tensor_scalar", R=[self.ident.t(), self.PB[l].t()], W=tiles if first else [], PW=[] if first else tiles,
                           out=st[:, (k - k0) * 128:(k - k0 + 1) * 128], in0=self.ident.h[:], scalar1=self.PB[l].h[:, k * 4 + j:k * 4 + j + 1], scalar2=None, op0=ALU.mult)
                    first = False
                ti = self.tidx[("cda" if part == 0 else "cdb") + str(j)]
                ne = (k1 - k0) * 128
                self.dma("pool", self.scr[l][ti][:, 0:ne], st[:, 0:ne], R=tiles, PW=[self.scr_t[l][ti]])

    def wtile(self, l, name):
        i = self.si
        assert self.sched[i] == (l, self.tidx[name]), (self.sched[i], l, name)
        self.si += 1
        hi = min(len(self.sched), i + self.NSLOT)
        self.ensure_prep0(hi + 16)
        NT = len(self.tnames)
        if self.deferred_done < len(self.deferred_all) and i >= (NT if self.NB > 1 else 0):
            calls_left = max(1, self.NB * NT - i - 2 * self.NSLOT)
            rate = -(-(len(self.deferred_all) - self.deferred_done) // calls_left)
            for _ in range(rate):
                if self.deferred_done < len(self.deferred_all):
                    self.deferred_all[self.deferred_done]()
                    self.deferred_done += 1
        while self.loaded < hi:
            k = self.loaded
            ll, ti = self.sched[k]
            if ll >= 1:
                self.run_deferred(1.0)
            sl = self.slots[k % self.NSLOT]
            ne = self.tsize[ti]
            self.dma("sp", sl.h[:, 0:ne], self.scr[ll][ti][:, 0:ne], R=[self.scr_t[ll][ti]], W=[sl.t()])
            self.loaded += 1
        return self.slots[i % self.NSLOT]

    def layer_seq_init(self, l, s):
        m = self.modT[l]
        PA = self.PA[l]
        R = [m.t(), PA.t()]
        sc = self.sc1
        self.V("dve", "scalar_tensor_tensor", R=R, W=[sc.t()], out=sc.h[:, 0, :], in0=m.h[:, 8:16, s], scalar=1.0, in1=PA.h[:, 0:8], op0=ALU.add, op1=ALU.mult)
        self.V("dve", "tensor_tensor", R=R, W=[], PW=[sc.t()], out=sc.h[:, 1, :], in0=m.h[:, 16:24, s], in1=PA.h[:, 8:16], op=ALU.mult)
        self.V("dve", "scalar_tensor_tensor", R=R, W=[], PW=[sc.t()], out=sc.h[:, 2, :], in0=m.h[:, 32:40, s], scalar=1.0, in1=PA.h[:, 16:24], op0=ALU.add, op1=ALU.mult)
        self.V("dve", "tensor_tensor", R=R, W=[], PW=[sc.t()], out=sc.h[:, 3, :], in0=m.h[:, 40:48, s], in1=PA.h[:, 24:32], op=ALU.mult)
        self.V("pool", "memset", R=[], W=[self.halo_sc.t()], ap=self.halo_sc.h[:], constant=0.0)
        self.V("pool", "memset", R=[], W=[self.halo_cf.t()], ap=self.halo_cf.h[:], constant=0.0)
        self.V("pool", "memset", R=[], W=[self.halo_ff.t()], ap=self.halo_ff.h[:], constant=0.0)

    def load_x_block(self, s, b):
        xT = self.xT
        for tt in range(4):
            t0 = b * TB + tt * 128
            for hf in range(2):
                st = self.fp.get()
                self.dma("act", st.h[:, 0:512], self.dr["x"][s, t0:t0 + 128, hf * 512:(hf + 1) * 512], W=[st.t()])
                p = self.psn()
                for q in range(4):
                    self.mm(p.h[:, q * 128:(q + 1) * 128], st.h[:, q * 128:(q + 1) * 128], self.ident.h[:], True, True, R=[st.t(), self.ident.t()], W=[p.t()])
                self.fp.free(st)
                for q in range(4):
                    c = hf * 4 + q
                    dst = xT.h[:, c, t0:t0 + 128]
                    src = p.h[:, q * 128:(q + 1) * 128]
                    W_, PW_ = ([xT.t((c, b))], []) if tt == 0 else ([], [xT.t((c, b))])
                    if q % 2 == 0:
                        self.E("dve", lambda h, dst=dst, src=src: h.tensor_copy(out=dst, in_=src), R=[p.t()], W=W_, PW=PW_)
                    else:
                        self.E("act", lambda h, dst=dst, src=src: h.activation(out=dst, in_=src, func=AF.Copy), R=[p.t()], W=W_, PW=PW_)
                self.psf(p)

    def store_x_block(self, s, b):
        for tt in range(4):
            t0 = b * TB + tt * 128
            for hf in range(2):
                p = self.psn()
                for q in range(4):
                    c = hf * 4 + q
                    self.mm(p.h[:, q * 128:(q + 1) * 128], self.xT.h[:, c, t0:t0 + 128], self.ident.h[:], True, True, R=[self.xT.t((c, b)), self.ident.t()], W=[p.t()])
                st = self.fp.get()
                self.V("dve", "tensor_copy", R=[p.t()], W=[st.t()], out=st.h[:, 0:512], in_=p.h[:, 0:512])
                self.psf(p)
                ev = self.dma("pool", self.out[s, t0:t0 + 128, hf * 512:(hf + 1) * 512], st.h[:, 0:512], R=[st.t()])
                self.out_evs.append(ev)
                self.fp.free(st)

    def rstd_from_ss(self, ps_ss, n, eps):
        r = self.fp.get()
        self.act(r.h[:, 0:512], ps_ss.h[:, 0:512], AF.Ln, R=[ps_ss.t(), self.small.t()], W=[r.t()], scale=float(1.0 / n), bias=self.epsb(eps))
        self.act(r.h[:, 0:512], r.h[:, 0:512], AF.Exp, R=[r.t()], W=[r.t()], scale=-0.5)
        return r

    def norm_in(self, l, s, b, which):
        xT, hT = self.xT, self.hT
        c0 = b * TB
        ss = self.psn()
        for c in range(KC):
            sq = self.bfp.get()
            self.act(sq.h[:], xT.h[:, c, c0:c0 + TB], AF.Square, R=[xT.t((c, b))], W=[sq.t()])
            self.mm(ss.h[:], self.ones.h[:], sq.h[:], c == 0, c == KC - 1, R=[self.ones.t(), sq.t()], W=[ss.t()])
            self.bfp.free(sq)
        rs = self.rstd_from_ss(ss, D, NORM_EPS)
        self.psf(ss)
        m = self.modT[l]
        for c in range(KC):
            t = self.fp.get()
            self.V("dve", "tensor_tensor", R=[xT.t((c, b)), rs.t()], W=[t.t()], out=t.h[:, 0:512], in0=xT.h[:, c, c0:c0 + TB], in1=rs.h[:, 0:512], op=ALU.mult)
            self.act(hT.h[:, c, :], t.h[:, 0:512], AF.Identity, R=[t.t(), self.sc1.t(), m.t()], W=[hT.t(c)],
                     scale=self.sc1.h[:, 2 * which, c:c + 1], bias=m.h[:, (24 * which) + c, s:s + 1])
            self.fp.free(t)
        self.fp.free(rs)

    def post_norm_residual(self, l, s, b, which, pss):
        xT = self.xT
        c0 = b * TB
        ys = []
        ss = self.psn() if len(self.psp.fl) > 0 else None
        own_ss = ss is not None
        for c in range(KC):
            sq = self.bfp.get()
            self.act(sq.h[:], pss[c].h[:], AF.Square, R=[pss[c].t()], W=[sq.t()])
            y = self.fp.get()
            if c >= NPOOLC:
                self.act(y.h[:, 0:512], pss[c].h[:], AF.Identity, R=[pss[c].t(), self.sc1.t()], W=[y.t()], scale=self.sc1.h[:, 2 * which + 1, c:c + 1])
            else:
                self.V("dve", "tensor_copy", R=[pss[c].t()], W=[y.t()], out=y.h[:, 0:512], in_=pss[c].h[:])
            ys.append(y)
            if ss is None:
                ss = pss[0]
            else:
                if c > 0 or own_ss:
                    pass
            self.mm(ss.h[:], self.ones.h[:], sq.h[:], c == 0, c == KC - 1, R=[self.ones.t(), sq.t()], W=[ss.t()])
            self.bfp.free(sq)
            if not (pss[c] is ss):
                self.psf(pss[c])
        rs = self.rstd_from_ss(ss, D, NORM_EPS)
        self.psf(ss)
        for c in range(NPOOLC, KC):
            y = ys[c]
            self.V("pool", "tensor_tensor", R=[y.t(), rs.t()], W=[y.t()], out=y.h[:, 0:512], in0=y.h[:, 0:512], in1=rs.h[:, 0:512], op=ALU.mult)
            self.V("pool", "tensor_tensor", R=[y.t(), xT.t((c, b))], W=[xT.t((c, b))], out=xT.h[:, c, c0:c0 + TB], in0=y.h[:, 0:512], in1=xT.h[:, c, c0:c0 + TB], op=ALU.add)
        for c in range(NPOOLC):
            y = ys[c]
            self.V("dve", "tensor_tensor", R=[y.t(), rs.t()], W=[y.t()], out=y.h[:, 0:512], in0=y.h[:, 0:512], in1=rs.h[:, 0:512], op=ALU.mult)
            self.V("dve", "scalar_tensor_tensor", R=[y.t(), self.sc1.t(), xT.t((c, b))], W=[xT.t((c, b))], out=xT.h[:, c, c0:c0 + TB],
                   in0=y.h[:, 0:512], scalar=self.sc1.h[:, 2 * which + 1, c:c + 1], in1=xT.h[:, c, c0:c0 + TB], op0=ALU.mult, op1=ALU.add)
        self.fp.free(rs, *ys)

    def rope_tables(self, s, b):
        c0 = b * TB
        pi_ = self.fp.get()
        piv = pi_.h[:, 0:512].bitcast(I32)
        self.dma("sp", piv, self.dr["positions"][s:s + 1, c0:c0 + TB].broadcast_to([128, TB]), W=[pi_.t()])
        y = self.fp.get()
        self.V("dve", "tensor_copy", R=[pi_.t()], W=[y.t()], out=y.h[:, 0:512], in_=piv)
        self.V("dve", "tensor_scalar", R=[y.t(), self.invf.t()], W=[y.t()], out=y.h[:, 0:512], in0=y.h[:, 0:512], scalar1=self.invf.h[:, 0:1], scalar2=None, op0=ALU.mult)
        self.fp.free(pi_)
        outs = []
        for shift in (0.25, 0.0):
            z = self.fp.get()
            self.V("dve", "tensor_scalar", R=[y.t()], W=[z.t()], out=z.h[:, 0:512], in0=y.h[:, 0:512], scalar1=float(shift), scalar2=None, op0=ALU.add)
            zi = self.fp.get()
            ziv = zi.h[:, 0:512].bitcast(I32)
            self.V("dve", "tensor_copy", R=[z.t()], W=[zi.t()], out=ziv, in_=z.h[:, 0:512])
            zf = self.fp.get()
            self.V("dve", "tensor_copy", R=[zi.t()], W=[zf.t()], out=zf.h[:, 0:512], in_=ziv)
            self.V("dve", "tensor_tensor", R=[z.t(), zf.t()], W=[z.t()], out=z.h[:, 0:512], in0=z.h[:, 0:512], in1=zf.h[:, 0:512], op=ALU.subtract)
            self.V("dve", "tensor_single_scalar", R=[z.t()], W=[zi.t()], out=zi.h[:, 0:512], in_=z.h[:, 0:512], scalar=0.5, op=ALU.is_gt)
            self.V("dve", "tensor_single_scalar", R=[z.t()], W=[zf.t()], out=zf.h[:, 0:512], in_=z.h[:, 0:512], scalar=-0.5, op=ALU.is_lt)
            self.V("dve", "tensor_tensor", R=[z.t(), zi.t()], W=[z.t()], out=z.h[:, 0:512], in0=z.h[:, 0:512], in1=zi.h[:, 0:512], op=ALU.subtract)
            self.V("dve", "tensor_tensor", R=[z.t(), zf.t()], W=[z.t()], out=z.h[:, 0:512], in0=z.h[:, 0:512], in1=zf.h[:, 0:512], op=ALU.add)
            o = self.fp.get()
            if shift == 0.0:
                self.act(o.h[:, 0:512], z.h[:, 0:512], AF.Sin, R=[z.t(), self.small.t()], W=[o.t()], scale=self.small.h[:, 2:3])
            else:
                self.act(o.h[:, 0:512], z.h[:, 0:512], AF.Sin, R=[z.t()], W=[o.t()], scale=float(2.0 * math.pi * (1.0 - 1e-6)))
            self.fp.free(z, zi, zf)
            outs.append(o)
        self.fp.free(y)
        return outs[0], outs[1]

    def attn_stream(self, nk, qb, kt_fn, q_ap, q_t, scale, pv_fn, R_k):
        LA = 3
        ring = [self.psn() for _ in range(LA + 1)]
        pend = {}
        cnt = [0]

        def qk(k):
            q0 = 128 * max(0, k - 4 * qb)
            p = ring[cnt[0] % (LA + 1)]
            cnt[0] += 1
            diag = k >= 4 * qb
            self.mm(p.h[:, q0:512], kt_fn(k), q_ap(q0), True, not diag, R=R_k + [q_t], W=[p.t()])
            if diag:
                self.mm(p.h[:, q0:q0 + 128], self.identb.h[:], self.tri.h[:], False, True, R=[self.identb.t(), self.tri.t()], W=[p.t()])
            pend[k] = (p, q0)
        for k in range(min(LA, nk)):
            qk(k)
        for k in range(nk):
            p, q0 = pend.pop(k)
            pt = self.pring.get()
            self.act(pt.h[:, q0:512], p.h[:, q0:512], AF.Exp, R=[p.t()], W=[pt.t()], scale=float(scale))
            if k + LA < nk:
                qk(k + LA)
            pv_fn(k, pt, q0, k == 0, k == nk - 1)
        self.psf(*ring)

    def mixer_block(self, l, s, b, skip_norm=False):
        NKB = 4 * (b + 1)
        c0 = b * TB
        hT, big = self.hT, self.big
        CQ, QM, QD = 8, 11, 19
        if not skip_norm:
            self.norm_in(l, s, b, 0)
        hR = [hT.t(c) for c in range(KC)]
        cs, sn = self.rope_tables(s, b)
        QZ = 4
        for h in range(4):
            for m in range(2):
                c = QZ + 2 * h + m
                zr = slice((1 - m) * 64, (1 - m) * 64 + 64)
                self.V("pool", "memset", R=[], W=[big.t(c)], ap=big.h[zr, c, :], constant=0.0)

        pv4 = [self.psn() for _ in range(4)]
        for half, nm in enumerate(["va", "vb"]):
            w = self.wtile(l, nm)
            wv = w.h[:, 0:2048].rearrange("p (k n) -> p k n", k=8)
            for tt in range(4):
                p = pv4[tt]
                for kc in range(8):
                    self.mm(p.h[:, half * 256:(half + 1) * 256], hT.h[:, kc, tt * 128:(tt + 1) * 128], wv[:, kc, :], kc == 0, kc == 7, R=[w.t()] + hR, W=[p.t()])
        for tt in range(4):
            kb = b * 4 + tt
            if tt % 2 == 0:
                self.act(self.Vda.h[:, kb, :], pv4[tt].h[:], AF.Copy, R=[pv4[tt].t()], W=[self.Vda.t(kb)])
            else:
                self.V("dve", "tensor_copy", R=[pv4[tt].t()], W=[self.Vda.t(kb)], out=self.Vda.h[:, kb, :], in_=pv4[tt].h[:])
        self.psf(*pv4)
        for half, nm in enumerate(["qk0a", "qk0b"]):
            w = self.wtile(l, nm)
            wv = w.h[:, 0:2048].rearrange("p (k n) -> p k n", k=8)
            for hh in range(2):
                h = half * 2 + hh
                p = self.psn()
                for kc in range(8):
                    self.mm(p.h[:], wv[:, kc, hh * 128:(hh + 1) * 128], hT.h[:, kc, :], kc == 0, kc == 7, R=[w.t()] + hR, W=[p.t()])
                self.act(big.h[0:64, QZ + 2 * h, :], p.h[0:64, :], AF.Copy, R=[p.t(), big.t(QZ + 2 * h)], W=[], PW=[big.t(QZ + 2 * h)])
                self.V("dve", "tensor_copy", R=[p.t(), big.t(QZ + 2 * h + 1)], W=[], PW=[big.t(QZ + 2 * h + 1)], out=big.h[64:128, QZ + 2 * h + 1, :], in_=p.h[64:128, :])
                self.psf(p)
        for half, nm in enumerate(["qk1a", "qk1b"]):
            w = self.wtile(l, nm)
            wv = w.h[:, 0:2048].rearrange("p (k n) -> p k n", k=8)
            for hh in range(2):
                h = half * 2 + hh
                p = self.psn()
                for kc in range(8):
                    self.mm(p.h[:], wv[:, kc, hh * 128:(hh + 1) * 128], hT.h[:, kc, :], kc == 0, kc == 7, R=[w.t()] + hR, W=[p.t()])
                self.V("dve", "tensor_copy", R=[p.t()], W=[self.KTda.t((h, b))], out=self.KTda.h[:, h, c0:c0 + TB], in_=p.h[:])
                self.psf(p)
        def da_finish(h, a):
            sq = self.bfp.get()
            self.act(sq.h[:], a.h[:, 0:512], AF.Square, R=[a.t()], W=[sq.t()])
            ss = self.psn()
            self.mm(ss.h[:], self.ones.h[:], sq.h[:], True, True, R=[self.ones.t(), sq.t()], W=[ss.t()])
            self.bfp.free(sq)
            rs = self.rstd_from_ss(ss, 128, NORM_EPS)
            self.psf(ss)
            self.V("dve", "scalar_tensor_tensor", R=[a.t(), rs.t(), self.gsub[l].t()], W=[big.t(h)], out=big.h[:, h, :], in0=a.h[:, 0:512],
                   scalar=self.gsub[l].h[:, 0:1], in1=rs.h[:, 0:512], op0=ALU.mult, op1=ALU.mult)
            self.fp.free(a, rs)
        pending = None
        for h in range(4):
            tms = []
            for m in range(2):
                rows = slice(m * 64, m * 64 + 64)
                O, L = self.psn(), self.psn()

                def pv(k, pt, q0, first, last, O=O, L=L, h=h):
                    self.mm(O.h[:, q0:512], self.Vda.h[:, k, h * 128:(h + 1) * 128], pt.h[:, q0:512], first, last, R=[self.Vda.t(k), pt.t()], W=[O.t()])
                    self.mm(L.h[:, q0:512], self.ones.h[:], pt.h[:, q0:512], first, last, R=[self.ones.t(), pt.t()], W=[L.t()])
                self.attn_stream(NKB, b,
                                 lambda k, h=h: self.KTda.h[:, h, k * 128:(k + 1) * 128],
                                 lambda q0, h=h, m=m: big.h[:, QZ + 2 * h + m, q0:512], big.t(QZ + 2 * h + m), 0.125, pv,
                                 R_k=[self.KTda.t((h, bb)) for bb in range(b + 1)])
                if m == 0 and pending is not None:
                    da_finish(*pending)
                    pending = None
                r = self.fp.get()
                self.act(r.h[:, 0:512], L.h[:], AF.Ln, R=[L.t()], W=[r.t()])
                self.act(r.h[:, 0:512], r.h[:, 0:512], AF.Exp, R=[r.t()], W=[r.t()], scale=-1.0)
                tm = self.fp.get()
                self.V("dve", "tensor_tensor", R=[O.t(), r.t()], W=[tm.t()], out=tm.h[:, 0:512], in0=O.h[:], in1=r.h[:, 0:512], op=ALU.mult)
                self.fp.free(r)
                self.psf(O, L)
                tms.append(tm)
            a = self.fp.get()
            self.V("dve", "scalar_tensor_tensor", R=[tms[0].t(), tms[1].t(), self.lamneg[l].t()], W=[a.t()], out=a.h[:, 0:512], in0=tms[1].h[:, 0:512],
                   scalar=self.lamneg[l].h[:, 0:1], in1=tms[0].h[:, 0:512], op0=ALU.mult, op1=ALU.add)
            self.fp.free(*tms)
            pending = (h, a)

        w = self.wtile(l, "c")
        wv = w.h[:, 0:3072].rearrange("p (k n) -> p k n", k=8)
        ssq = self.psn()
        for j in range(3):
            p = self.psn()
            for kc in range(8):
                self.mm(p.h[:], wv[:, kc, j * 128:(j + 1) * 128], hT.h[:, kc, :], kc == 0, kc == 7, R=[w.t()] + hR, W=[p.t()])
            if j == 0 and pending is not None:
                da_finish(*pending)
                pending = None
            sq = self.bfp.get()
            self.act(sq.h[:], p.h[:], AF.Square, R=[p.t()], W=[sq.t()])
            self.V("dve", "tensor_scalar", R=[p.t(), self.PA[l].t()], W=[big.t(CQ + j)], out=big.h[:, CQ + j, :], in0=p.h[:], scalar1=self.PA[l].h[:, 81 + j:82 + j], scalar2=None, op0=ALU.mult)
            self.psf(p)
            self.mm(ssq.h[:], self.ones.h[:], sq.h[:], j == 0, j == 2, R=[self.ones.t(), sq.t()], W=[ssq.t()])
            self.bfp.free(sq)
        rq = self.rstd_from_ss(ssq, 384, NORM_EPS)
        self.psf(ssq)
        w2 = self.wtile(l, "c2")
        w2v = w2.h[:, 0:2560].rearrange("p (k n) -> p k n", k=8)
        pk = [self.psn(), self.psn()]
        for j in range(2):
            for kc in range(8):
                self.mm(pk[j].h[:], w2v[:, kc, j * 128:(j + 1) * 128], hT.h[:, kc, :], kc == 0, kc == 7, R=[w2.t()] + hR, W=[pk[j].t()])
        pr = self.psn()
        for kc in range(8):
            self.mm(pr.h[64:128, :], w2v[:, kc, 256:320], hT.h[:, kc, :], kc == 0, kc == 7, R=[w2.t()] + hR, W=[pr.t()])
        ssk = self.psn()
        for j in range(2):
            sq = self.bfp.get()
            self.act(sq.h[:], pk[j].h[:], AF.Square, R=[pk[j].t()], W=[sq.t()])
            self.mm(ssk.h[:], self.ones.h[:], sq.h[:], j == 0, j == 1, R=[self.ones.t(), sq.t()], W=[ssk.t()])
            self.bfp.free(sq)
        rk = self.rstd_from_ss(ssk, 256, NORM_EPS)
        self.psf(ssk)
        for j in range(2):
            self.V("dve", "scalar_tensor_tensor", R=[pk[j].t(), rk.t(), self.PA[l].t()], W=[self.lat.t((j, b))], out=self.lat.h[:, j, c0:c0 + TB], in0=pk[j].h[:],
                   scalar=self.PA[l].h[:, 84 + j:85 + j], in1=rk.h[:, 0:512], op0=ALU.mult, op1=ALU.mult)
        self.psf(*pk)
        self.fp.free(rk)
        t1, t2 = self.fp.get(), self.fp.get()
        self.V("dve", "tensor_tensor", R=[pr.t(), sn.t()], W=[t1.t()], out=t1.h[64:96, 0:512], in0=pr.h[96:128, :], in1=sn.h[96:128, 0:512], op=ALU.mult)
        self.V("dve", "tensor_tensor", R=[pr.t(), cs.t()], W=[t2.t()], out=t2.h[64:96, 0:512], in0=pr.h[64:96, :], in1=cs.h[64:96, 0:512], op=ALU.mult)
        self.psf(pr)
        for i in range(2):
            self.V("pool", "tensor_tensor", R=[t1.t(), t2.t()], W=[self.KTw[i].t(("r", b))], out=self.KTw[i].h[64:96, c0:c0 + TB], in0=t1.h[64:96, 0:512], in1=t2.h[64:96, 0:512], op=ALU.add)
        self.fp.free(t1, t2)
        csr, snr = self.fp.get(), self.fp.get()
        self.V("dve", "tensor_tensor", R=[cs.t(), rq.t()], W=[csr.t()], out=csr.h[64:96, 0:512], in0=cs.h[64:96, 0:512], in1=rq.h[64:96, 0:512], op=ALU.mult)
        self.V("dve", "tensor_tensor", R=[sn.t(), rq.t()], W=[snr.t()], out=snr.h[96:128, 0:512], in0=sn.h[96:128, 0:512], in1=rq.h[96:128, 0:512], op=ALU.mult)
        self.fp.free(cs, sn)
        wq = self.wtile(l, "uq")
        wqv = wq.h[:, 0:3072].rearrange("p (k h n) -> p k h n", k=3, h=8)
        for h in range(8):
            p = self.psn()
            for kc in range(3):
                self.mm(p.h[:], wqv[:, kc, h, :], big.h[:, CQ + kc, :], kc == 0, kc == 2, R=[wq.t(), big.t(CQ + kc)], W=[p.t()])
            qt = big.t(QM + h)
            self.V("dve", "tensor_tensor", R=[p.t(), rq.t()], W=[qt], out=big.h[0:64, QM + h, :], in0=p.h[0:64, :], in1=rq.h[0:64, 0:512], op=ALU.mult)
            u1, u2 = self.fp.get(), self.fp.get()
            self.V("dve", "tensor_tensor", R=[p.t(), snr.t()], W=[u1.t()], out=u1.h[64:96, 0:512], in0=p.h[96:128, :], in1=snr.h[96:128, 0:512], op=ALU.mult)
            self.V("dve", "tensor_tensor", R=[p.t(), csr.t()], W=[u2.t()], out=u2.h[64:96, 0:512], in0=p.h[64:96, :], in1=csr.h[64:96, 0:512], op=ALU.mult)
            self.psf(p)
            self.V("pool", "tensor_tensor", R=[u1.t(), u2.t()], W=[], PW=[qt], out=big.h[64:96, QM + h, :], in0=u1.h[64:96, 0:512], in1=u2.h[64:96, 0:512], op=ALU.add)
            self.fp.free(u1, u2)
        self.fp.free(rq, csr, snr)

        wk = self.wtile(l, "ukv")
        wkv = wk.h[:, 0:2048].rearrange("p (k t h n) -> p k t h n", k=2, t=2, h=8)
        latR = [self.lat.t((j, bb)) for j in range(2) for bb in range(b + 1)]

        def recompute(h):
            KT, VT = self.KTw[h % 2], self.VTw[h % 2]
            for kg in range(b + 1):
                p = self.psn()
                for kc in range(2):
                    self.mm(p.h[0:64, :], wkv[:, kc, 0, h, :], self.lat.h[:, kc, kg * TB:(kg + 1) * TB], kc == 0, kc == 1, R=[wk.t()] + latR, W=[p.t()])
                self.act(KT.h[0:64, kg * TB:(kg + 1) * TB], p.h[0:64, :], AF.Copy, R=[p.t()], W=[KT.t(("n", kg))])
                self.psf(p)
            for g in range((NKB + 7) // 8):
                p = self.psn()
                nk = min(8, NKB - g * 8)
                for q in range(nk):
                    kb = g * 8 + q
                    for kc in range(2):
                        self.mm(p.h[:, q * 64:(q + 1) * 64], self.lat.h[:, kc, kb * 128:(kb + 1) * 128], wkv[:, kc, 1, h, :], kc == 0, kc == 1,
                                R=[wk.t()] + latR, W=[p.t()])
                self.V("dve", "tensor_copy", R=[p.t()], W=[VT.t(g)], out=VT.h[:, g * 8:g * 8 + nk, 0:64],
                       in_=p.h[:, 0:nk * 64].rearrange("p (q n) -> p q n", n=64))
                self.psf(p)
        recompute(0)
        for h in range(8):
            if h + 1 < 8:
                recompute(h + 1)
            KT, VT = self.KTw[h % 2], self.VTw[h % 2]
            O = self.psn()

            def pv(k, pt, q0, first, last, O=O, VT=VT):
                self.mm(O.h[:, q0:512], VT.h[:, k, :], pt.h[:, q0:512], first, last, R=[VT.t(k // 8), pt.t()], W=[O.t()])
            self.attn_stream(NKB, b,
                             lambda k, KT=KT: KT.h[:, k * 128:(k + 1) * 128],
                             lambda q0, h=h: big.h[:, QM + h, q0:512], big.t(QM + h), 96.0 ** -0.5, pv,
                             R_k=[KT.t("z")] + [KT.t(("n", kg)) for kg in range(b + 1)] + [KT.t(("r", bb)) for bb in range(b + 1)])
            r = self.fp.get()
            self.act(r.h[0:64, 0:512], O.h[64:128, :], AF.Ln, R=[O.t()], W=[r.t()])
            self.act(r.h[0:64, 0:512], r.h[0:64, 0:512], AF.Exp, R=[r.t()], W=[r.t()], scale=-1.0)
            ro = (h % 2) * 64
            bt = big.t(4 + h // 2)
            self.V("dve", "tensor_tensor", R=[O.t(), r.t()], W=[bt] if h % 2 == 0 else [], PW=[] if h % 2 == 0 else [bt],
                   out=big.h[ro:ro + 64, 4 + h // 2, :], in0=O.h[0:64, :], in1=r.h[0:64, 0:512], op=ALU.mult)
            self.psf(O)
            self.fp.free(r)

        PA = self.PA[l]
        for j in range(4):
            w = self.wtile(l, f"sc{j}")
            wv = w.h[:, 0:3072].rearrange("p (q k n) -> p q k n", q=3, k=8)
            pp = []
            for q in range(3):
                p = self.psn()
                for kc in range(8):
                    self.mm(p.h[:], wv[:, q, kc, :], hT.h[:, kc, :], kc == 0, kc == 7, R=[w.t()] + hR, W=[p.t()])
                pp.append(p)
            us = self.fp.get()
            self.act(us.h[:, 0:512], pp[0].h[:], AF.Copy, R=[pp[0].t()], W=[us.t()])
            ub = self.fp.get()
            self.V("pool", "tensor_copy", R=[self.halo_sc.t()], W=[ub.t()], out=ub.h[:, 0:2], in_=self.halo_sc.h[:, j, :])
            self.V("dve", "tensor_tensor", R=[pp[1].t(), us.t()], W=[], PW=[ub.t()], out=ub.h[:, 2:514], in0=pp[1].h[:], in1=us.h[:, 0:512], op=ALU.mult)
            self.fp.free(us)
            acc = self.conv_taps(ub, 3, lambda k, j=j: PA.h[:, 86 + k * 4 + j:87 + k * 4 + j], [PA.t()], None, "dve")
            self.V("pool", "tensor_copy", R=[ub.t()], W=[], PW=[self.halo_sc.t()], out=self.halo_sc.h[:, j, :], in_=ub.h[:, 512:514])
            self.V("dve", "tensor_tensor", R=[pp[2].t(), acc.t()], W=[big.t(8 + j)], out=big.h[:, 8 + j, :], in0=pp[2].h[:], in1=acc.h[:, 0:512], op=ALU.mult)
            self.psf(*pp)
            self.fp.free(ub, acc)

        PB = self.PB[l]
        vs = []
        ubs = []
        for j in range(4):
            w = self.wtile(l, f"cf{j}")
            wv = w.h[:, 0:2048].rearrange("p (q k n) -> p q k n", q=2, k=8)
            pa, pg = self.psn(), self.psn()
            for kc in range(8):
                self.mm(pa.h[:], wv[:, 0, kc, :], hT.h[:, kc, :], kc == 0, kc == 7, R=[w.t()] + hR, W=[pa.t()])
            for kc in range(8):
                self.mm(pg.h[:], wv[:, 1, kc, :], hT.h[:, kc, :], kc == 0, kc == 7, R=[w.t()] + hR, W=[pg.t()])
            sg = self.fp.get()
            self.act(sg.h[:, 0:512], pg.h[:], AF.Sigmoid, R=[pg.t()], W=[sg.t()])
            ubf = self.fp.get()
            ub = ubf.h[:, 0:272].bitcast(BF16)
            self.V("pool", "tensor_copy", R=[self.halo_cf.t()], W=[ubf.t()], out=ub[:, 0:30], in_=self.halo_cf.h[:, j, :])
            self.V("dve", "tensor_tensor", R=[pa.t(), sg.t()], W=[], PW=[ubf.t()], out=ub[:, 30:542], in0=pa.h[:], in1=sg.h[:, 0:512], op=ALU.mult)
            self.psf(pa, pg)
            self.fp.free(sg)
            self.V("pool", "tensor_copy", R=[ubf.t()], W=[], PW=[self.halo_cf.t()], out=self.halo_cf.h[:, j, :], in_=ub[:, 512:542])
            ubs.append((ubf, ub))
        for j in range(4):
            ubf, ub = ubs[j]
            acc = self.psn()
            wa = self.wtile(l, f"cda{j}")
            for k in range(16):
                self.mm(acc.h[:], wa.h[:, k * 128:(k + 1) * 128], ub[:, k:k + 512], k == 0, False, R=[wa.t(), ubf.t()], W=[acc.t()])
            wb = self.wtile(l, f"cdb{j}")
            for k in range(16, 31):
                self.mm(acc.h[:], wb.h[:, (k - 16) * 128:(k - 15) * 128], ub[:, k:k + 512], False, k == 30, R=[wb.t(), ubf.t()], W=[acc.t()])
            v = self.fp.get()
            self.V("dve", "tensor_scalar", R=[acc.t(), PA.t()], W=[v.t()], out=v.h[:, 0:512], in0=acc.h[:], scalar1=PA.h[:, 98 + j:99 + j], scalar2=None, op0=ALU.add)
            self.psf(acc)
            self.fp.free(ubf)
            vs.append(v)
        s1, s2 = self.psn(), self.psn()
        for j in range(4):
            vb = self.bfp.get()
            self.act(vb.h[:], vs[j].h[:, 0:512], AF.Copy, R=[vs[j].t()], W=[vb.t()])
            sq = self.bfp.get()
            self.act(sq.h[:], vs[j].h[:, 0:512], AF.Square, R=[vs[j].t()], W=[sq.t()])
            self.mm(s1.h[:], self.ones.h[:], vb.h[:], j == 0, j == 3, R=[self.ones.t(), vb.t()], W=[s1.t()])
            self.mm(s2.h[:], self.ones.h[:], sq.h[:], j == 0, j == 3, R=[self.ones.t(), sq.t()], W=[s2.t()])
            self.bfp.free(vb, sq)
        mean, msq = self.fp.get(), self.fp.get()
        self.V("dve", "tensor_scalar", R=[s1.t()], W=[mean.t()], out=mean.h[:, 0:512], in0=s1.h[:], scalar1=float(1.0 / 512), scalar2=None, op0=ALU.mult)
        self.V("dve", "tensor_tensor", R=[mean.t()], W=[msq.t()], out=msq.h[:, 0:512], in0=mean.h[:, 0:512], in1=mean.h[:, 0:512], op=ALU.mult)
        var = self.fp.get()
        self.V("dve", "scalar_tensor_tensor", R=[s2.t(), msq.t()], W=[var.t()], out=var.h[:, 0:512], in0=s2.h[:], scalar=float(1.0 / 512), in1=msq.h[:, 0:512], op0=ALU.mult, op1=ALU.subtract)
        self.psf(s1, s2)
        self.act(msq.h[:, 0:512], var.h[:, 0:512], AF.Ln, R=[var.t(), self.small.t()], W=[msq.t()], bias=self.epsb(LN_EPS), scale=1.0)
        self.act(msq.h[:, 0:512], msq.h[:, 0:512], AF.Exp, R=[msq.t()], W=[msq.t()], scale=-0.5)
        rstd = msq
        self.fp.free(var)
        for j in range(4):
            v = vs[j]
            self.V("dve", "tensor_tensor", R=[v.t(), mean.t()], W=[v.t()], out=v.h[:, 0:512], in0=v.h[:, 0:512], in1=mean.h[:, 0:512], op=ALU.subtract)
            self.V("dve", "tensor_tensor", R=[v.t(), rstd.t()], W=[v.t()], out=v.h[:, 0:512], in0=v.h[:, 0:512], in1=rstd.h[:, 0:512], op=ALU.mult)
            self.act(big.h[:, 12 + j, :], v.h[:, 0:512], AF.Silu, R=[v.t(), PA.t()], W=[big.t(12 + j)], scale=PA.h[:, 102 + j:103 + j], bias=PA.h[:, 106 + j:107 + j])
        self.fp.free(mean, rstd, *vs)

        for j in range(8):
            m = None
            for half, nm in enumerate([f"ma{j}", f"mb{j}"]):
                w = self.wtile(l, nm)
                gv = w.h[:, 0:2048].rearrange("p (i k n) -> p i k n", i=2, k=8)
                bv = w.h[:, 2048:3072].rearrange("p (i k n) -> p i k n", i=2, k=4)
                for ii in range(2):
                    i = half * 2 + ii
                    py, pg = self.psn(), self.psn()
                    for kc in range(8):
                        self.mm(pg.h[:], gv[:, ii, kc, :], hT.h[:, kc, :], kc == 0, kc == 7, R=[w.t()] + hR, W=[pg.t()])
                    for kc in range(4):
                        self.mm(py.h[:], bv[:, ii, kc, :], big.h[:, 4 * i + kc, :], kc == 0, kc == 3, R=[w.t(), big.t(4 * i + kc)], W=[py.t()])
                    sg = self.fp.get()
                    self.act(sg.h[:, 0:512], pg.h[:], AF.Sigmoid, R=[pg.t()], W=[sg.t()])
                    t = self.fp.get()
                    self.V("dve", "tensor_tensor", R=[py.t(), sg.t()], W=[t.t()], out=t.h[:, 0:512], in0=py.h[:], in1=sg.h[:, 0:512], op=ALU.mult)
                    self.psf(py, pg)
                    self.fp.free(sg)
                    if m is None:
                        m = t
                    elif i < 3:
                        self.V("pool", "tensor_tensor", R=[m.t(), t.t()], W=[t.t()], out=t.h[:, 0:512], in0=m.h[:, 0:512], in1=t.h[:, 0:512], op=ALU.add)
                        self.fp.free(m)
                        m = t
                    else:
                        self.V("pool", "tensor_tensor", R=[m.t(), t.t()], W=[big.t(16 + j)], out=big.h[:, 16 + j, :], in0=m.h[:, 0:512], in1=t.h[:, 0:512], op=ALU.add)
                        self.fp.free(m, t)

        pss = []
        for t in range(4):
            w = self.wtile(l, f"mix{t}")
            wv = w.h[:, 0:2048].rearrange("p (k n) -> p k n", k=8)
            for q in range(2):
                p = self.psn()
                for kc in range(8):
                    self.mm(p.h[:], wv[:, kc, q * 128:(q + 1) * 128], big.h[:, 16 + kc, :], kc == 0, kc == 7, R=[w.t(), big.t(16 + kc)], W=[p.t()])
                pss.append(p)
        self.post_norm_residual(l, s, b, 0, pss)

    def conv_taps(self, ub, K, wfn, Rw, bias, en):
        acc = self.fp.get()
        k = K - 1
        if bias is None:
            self.V(en, "tensor_scalar", R=[ub.t()] + Rw, W=[acc.t()], out=acc.h[:, 0:512], in0=ub.h[:, k:k + 512], scalar1=wfn(k), scalar2=None, op0=ALU.mult)
        else:
            self.V(en, "tensor_scalar", R=[ub.t()] + Rw, W=[acc.t()], out=acc.h[:, 0:512], in0=ub.h[:, k:k + 512], scalar1=wfn(k), scalar2=bias, op0=ALU.mult, op1=ALU.add)
        for k in range(K - 2, -1, -1):
            self.V(en, "scalar_tensor_tensor", R=[ub.t(), acc.t()] + Rw, W=[acc.t()], out=acc.h[:, 0:512], in0=ub.h[:, k:k + 512], scalar=wfn(k), in1=acc.h[:, 0:512], op0=ALU.mult, op1=ALU.add)
        return acc

    def ffn_gate_mul(self, g, pb, sl, c):
        self.V("dve", "tensor_tensor", R=[pb.t(), sl.t()], W=[g.t(c)], out=g.h[:, c, :], in0=pb.h[:], in1=sl.h[:, 0:512], op=ALU.mult)
        self.psf(pb)
        self.fp.free(sl)

    def ffn_block(self, l, s, b, hoist=False):
        hT = self.hT
        self.norm_in(l, s, b, 1)
        hR = [hT.t(c) for c in range(KC)]
        PC = self.PC[l]
        g = self.big
        ffn_pending = None
        for c in range(FC):
            w = self.wtile(l, f"up{c}")
            wv = w.h[:, 0:2048].rearrange("p (q k n) -> p q k n", q=2, k=8)
            pa, pb = self.psn(), self.psn()
            for kc in range(8):
                self.mm(pa.h[:], wv[:, 0, kc, :], hT.h[:, kc, :], kc == 0, kc == 7, R=[w.t()] + hR, W=[pa.t()])
            for kc in range(8):
                self.mm(pb.h[:], wv[:, 1, kc, :], hT.h[:, kc, :], kc == 0, kc == 7, R=[w.t()] + hR, W=[pb.t()])
            ab = self.fp.get()
            self.V("pool", "tensor_copy", R=[self.halo_ff.t()], W=[ab.t()], out=ab.h[:, 0:2], in_=self.halo_ff.h[:, c, :])
            self.act(ab.h[:, 2:514], pa.h[:], AF.Copy, R=[pa.t()], W=[], PW=[ab.t()])
            acc = self.conv_taps(ab, 3, lambda k, c=c: PC.h[:, k * FC + c:k * FC + c + 1], [PC.t()], None, "dve")
            self.V("pool", "tensor_copy", R=[ab.t()], W=[], PW=[self.halo_ff.t()], out=self.halo_ff.h[:, c, :], in_=ab.h[:, 512:514])
            self.fp.free(ab)
            sl = self.fp.get()
            self.act(sl.h[:, 0:512], acc.h[:, 0:512], AF.Silu, R=[acc.t()], W=[sl.t()])
            self.psf(pa)
            self.fp.free(acc)
            if ffn_pending is not None:
                self.ffn_gate_mul(g, *ffn_pending)
            ffn_pending = (pb, sl, c)
        self.ffn_gate_mul(g, *ffn_pending)
        pss = []
        for j in range(8):
            if hoist and j == 4:
                self.norm_in(l, s, b + 1, 0)
            w = self.wtile(l, f"dn{j}")
            wv = w.h[:, 0:FC * 128].rearrange("p (k n) -> p k n", k=FC)
            p = self.psn()
            for kc in range(FC):
                self.mm(p.h[:], wv[:, kc, :], g.h[:, kc, :], kc == 0, kc == FC - 1, R=[w.t(), g.t(kc)], W=[p.t()])
            pss.append(p)
        self.post_norm_residual(l, s, b, 1, pss)


_CACHE = {}


def get_program(S, NSEQ):
    key = (S, NSEQ)
    if key not in _CACHE:
        kb = KB(S, NSEQ)
        _CACHE[key] = kb.build()
    return _CACHE[key]


def kernel(**inputs):
    x = np.ascontiguousarray(np.asarray(inputs["x"], dtype=np.float32))
    B, S, _ = x.shape
    ncores = 8 if B % 8 == 0 else (B if B < 8 else 1)
    NSEQ = B // ncores
    nc = get_program(S, NSEQ)
    c = np.ascontiguousarray(np.asarray(inputs["c"], dtype=np.float32))
    pos = np.ascontiguousarray(np.asarray(inputs["positions"], dtype=np.int32))
    wts = {n: np.ascontiguousarray(np.asarray(inputs[n], dtype=np.float32)) for n in WNAMES}
    in_maps = []
    for i in range(ncores):
        m = {"x": x[i * NSEQ:(i + 1) * NSEQ], "c": c[i * NSEQ:(i + 1) * NSEQ], "positions": pos[i * NSEQ:(i + 1) * NSEQ]}
        m.update(wts)
        in_maps.append(m)
    res = run_bass_kernel_spmd(nc, in_maps, core_ids=list(range(ncores)))
    out = np.concatenate([np.asarray(r["out"]) for r in res.results], axis=0)
    return out.astype(np.float32)
```

```python
import math
import numpy as np
import concourse.bass as bass
import concourse.mybir as mybir
from concourse.bass_utils import run_bass_kernel_spmd

F32 = mybir.dt.float32
BF16 = mybir.dt.bfloat16
I32 = mybir.dt.int32
AF = mybir.ActivationFunctionType
ALU = mybir.AluOpType
AX = mybir.AxisListType

D = 1024
KC = 8
TB = 512
IN_W = 8864
FFN = 2816
FC = 22
NORM_EPS = 1e-6
LN_EPS = 1e-5
OFF_DAQ, OFF_DAK, OFF_DAV, OFF_CQ, OFF_CKV, OFF_KR = 0, 512, 1024, 1536, 1920, 2176
OFF_SCU, OFF_SCB, OFF_SCC, OFF_CFA, OFF_CFG, OFF_GATE = 2208, 2720, 3232, 3744, 4256, 4768
SLOT_E = 3072
NFP = 10
NPOOLC = 5
WNAMES = ["w_ada", "b_ada", "g_pre_mix", "w_in", "da_lam_q1", "da_lam_k1", "da_lam_q2", "da_lam_k2",
          "da_g_subln", "da_w_o", "mla_g_cq", "mla_w_uq", "mla_g_ckv", "mla_w_ukv", "mla_w_o",
          "sc_conv", "sc_w_o", "cf_conv", "cf_conv_b", "cf_ln_g", "cf_ln_b", "cf_w_o",
          "w_mix_out", "g_post_mix", "g_pre_ffn", "w_up", "ffn_conv", "w_down", "g_post_ffn"]
WSHAPES = {
    "w_ada": [2, 1024, 6144], "b_ada": [2, 6144], "g_pre_mix": [2, 1024], "w_in": [2, 1024, IN_W],
    "da_lam_q1": [2, 64], "da_lam_k1": [2, 64], "da_lam_q2": [2, 64], "da_lam_k2": [2, 64],
    "da_g_subln": [2, 128], "da_w_o": [2, 512, 1024], "mla_g_cq": [2, 384], "mla_w_uq": [2, 384, 768],
    "mla_g_ckv": [2, 256], "mla_w_ukv": [2, 256, 1024], "mla_w_o": [2, 512, 1024],
    "sc_conv": [2, 3, 512], "sc_w_o": [2, 512, 1024], "cf_conv": [2, 31, 512], "cf_conv_b": [2, 512],
    "cf_ln_g": [2, 512], "cf_ln_b": [2, 512], "cf_w_o": [2, 512, 1024], "w_mix_out": [2, 1024, 1024],
    "g_post_mix": [2, 1024], "g_pre_ffn": [2, 1024], "w_up": [2, 1024, 2 * FFN], "ffn_conv": [2, 3, FFN],
    "w_down": [2, FFN, 1024], "g_post_ffn": [2, 1024],
}


class Sem:
    def __init__(self, h, key):
        self.h = h
        self.key = key
        self.count = 0


class Ev:
    __slots__ = ("sem", "val", "eng")

    def __init__(self, sem, val, eng):
        self.sem = sem
        self.val = val
        self.eng = eng


class Tile:
    __slots__ = ("w", "r", "excl")

    def __init__(self):
        self.w = []
        self.r = {}
        self.excl = False


class Eng:
    def __init__(self, name, h, sem):
        self.name = name
        self.h = h
        self.sem = sem
        self.count = 0
        self.waited = {}


class Buf:
    def __init__(self, h):
        self.h = h
        self.tiles = {}

    def t(self, key=0):
        tl = self.tiles.get(key)
        if tl is None:
            tl = Tile()
            self.tiles[key] = tl
        return tl


class Ring:
    def __init__(self, bufs):
        self.bufs = bufs
        self.i = 0

    def get(self):
        b = self.bufs[self.i % len(self.bufs)]
        self.i += 1
        return b


class Pool:
    def __init__(self, bufs):
        self.fl = list(bufs)
        self.n = len(bufs)

    def get(self):
        assert self.fl, "pool exhausted"
        return self.fl.pop(0)

    def free(self, *bs):
        for b in bs:
            assert b not in self.fl
            self.fl.append(b)


class KB:
    def __init__(self, S, NSEQ, NL=2, NSLOT=3, NDS=24):
        self.S, self.NSEQ, self.NL, self.NSLOT = S, NSEQ, NL, NSLOT
        self.NB = S // TB
        nc = bass.Bass("TRN2", target_bir_lowering=False)
        self.nc = nc
        nsem = [0]

        def mksem(name):
            s = Sem(nc.alloc_semaphore(name), nsem[0])
            nsem[0] += 1
            return s
        self.eng = {
            "pe": Eng("pe", nc.tensor, mksem("s_pe")),
            "act": Eng("act", nc.scalar, mksem("s_act")),
            "dve": Eng("dve", nc.vector, mksem("s_dve")),
            "pool": Eng("pool", nc.gpsimd, mksem("s_pool")),
            "sp": Eng("sp", nc.sync, mksem("s_sp")),
        }
        self.dsems = {"sp": [mksem(f"s_dma{i}") for i in range(NDS)], "pool": [mksem(f"s_pdma{i}") for i in range(NDS)],
                      "act": [mksem(f"s_adma{i}") for i in range(8)]}
        self.dsi = {"sp": 0, "pool": 0, "act": 0}
        self.out_evs = []

    def _wait(self, e, evs):
        need = {}
        for ev in evs:
            k = ev.sem.key
            if k not in need or need[k].val < ev.val:
                need[k] = ev
        for k, ev in need.items():
            if e.waited.get(k, 0) < ev.val:
                e.h.wait_ge(ev.sem.h, ev.val)
                e.waited[k] = ev.val

    def _collect(self, e, R, W, PW):
        evs = []
        for t in R:
            evs.extend(t.w)
            if t.excl:
                for ev in t.r.values():
                    if ev.eng != e.name:
                        evs.append(ev)
        skip = "pe" if e.name == "pe" else None
        for t in W:
            for ev in t.w:
                if ev.eng != skip:
                    evs.append(ev)
            for ev in t.r.values():
                if ev.eng != skip:
                    evs.append(ev)
        for t in PW:
            for ev in t.r.values():
                if ev.eng != skip:
                    evs.append(ev)
        return evs

    def _record(self, ev, R, W, PW):
        rk = ("d", ev.sem.key) if ev.eng == "dma" else ev.eng
        for t in R:
            t.r[rk] = ev
        for t in W:
            t.w = [ev]
            t.r = {}
        for t in PW:
            t.w.append(ev)

    def E(self, en, fn, R=(), W=(), PW=(), sig=True):
        e = self.eng[en]
        self._wait(e, self._collect(e, R, W, PW))
        ins = fn(e.h)
        if sig:
            e.count += 1
            ins.then_inc(e.sem.h, 1)
            ev = Ev(e.sem, e.count, en)
        else:
            ev = Ev(e.sem, e.count + 1, en)
        self._record(ev, R, W, PW)
        return ev

    def dma(self, q, out, in_, R=(), W=(), PW=()):
        e = self.eng[q]
        self._wait(e, self._collect(e, R, W, PW))
        sem = self.dsems[q][self.dsi[q] % len(self.dsems[q])]
        self.dsi[q] += 1
        if e.waited.get(sem.key, 0) < sem.count:
            e.h.wait_ge(sem.h, sem.count)
            e.waited[sem.key] = sem.count
        ins = e.h.dma_start(out=out, in_=in_)
        ins.then_inc(sem.h, 16)
        sem.count += 16
        ev = Ev(sem, sem.count, "dma")
        self._record(ev, R, W, PW)
        return ev

    def mm(self, out, lhsT, rhs, start, stop, R, W, sig=None):
        sig = True
        return self.E("pe", lambda h: h.matmul(out, lhsT, rhs, start=start, stop=stop), R=R, W=W, sig=sig)

    def tr(self, out, in_, ident, R, W):
        return self.E("pe", lambda h: h.transpose(out, in_, ident), R=R, W=W, sig=True)

    def act(self, out, in_, func, R, W, PW=(), **kw):
        return self.E("act", lambda h: h.activation(out=out, in_=in_, func=func, **kw), R=R, W=W, PW=PW)

    def V(self, en, meth, R, W, PW=(), **kw):
        return self.E(en, lambda h: getattr(h, meth)(**kw), R=R, W=W, PW=PW)

    def build(self):
        nc, S, NSEQ, NL, NB = self.nc, self.S, self.NSEQ, self.NL, self.NB
        dr = {}
        dr["x"] = nc.dram_tensor("x", [NSEQ, S, D], F32, kind="ExternalInput").ap()
        dr["c"] = nc.dram_tensor("c", [NSEQ, D], F32, kind="ExternalInput").ap()
        dr["positions"] = nc.dram_tensor("positions", [NSEQ, S], I32, kind="ExternalInput").ap()
        for n in WNAMES:
            dr[n] = nc.dram_tensor(n, WSHAPES[n], F32, kind="ExternalInput").ap()
        self.dr = dr
        self.out = nc.dram_tensor("out", [NSEQ, S, D], F32, kind="ExternalOutput").ap()

        names = ["va", "vb", "qk0a", "qk0b", "qk1a", "qk1b", "c", "c2", "uq", "ukv",
                 "sc0", "sc1", "sc2", "sc3"]
        names += [f"cf{j}" for j in range(4)]
        for j in range(4):
            names += [f"cda{j}", f"cdb{j}"]
        for j in range(8):
            names += [f"ma{j}", f"mb{j}"]
        names += [f"mix{t}" for t in range(4)] + [f"up{t}" for t in range(FC)] + [f"dn{j}" for j in range(8)]
        self.tnames = names
        self.tidx = {n: i for i, n in enumerate(names)}

        def tsz(n):
            if n in ("c", "uq") or n[:2] in ("sc", "ma", "mb"):
                return 3072
            if n == "c2":
                return 2560
            if n[:2] == "dn":
                return FC * 128
            if n[:3] == "cdb":
                return 15 * 128
            return 2048
        self.tsize = [tsz(n) for n in names]
        NT = len(names)
        self.scr = [nc.dram_tensor(f"scr{l}", [NT, 128, SLOT_E], BF16, kind="Internal").ap() for l in range(NL)]
        self.scr_t = [[Tile() for _ in range(NT)] for _ in range(NL)]

        A = nc.alloc_sbuf_tensor
        self.xT = Buf(A("xT", [128, KC, S], F32))
        self.KTda = Buf(A("KTda", [128, 4, S], BF16))
        self.Vda = Buf(A("Vda", [128, S // 128, 512], BF16))
        self.lat = Buf(A("lat", [128, 2, S], BF16))
        self.KTw = [Buf(A(f"KTw{i}", [128, S], BF16)) for i in range(2)]
        self.VTw = [Buf(A(f"VTw{i}", [128, S // 128, 128], BF16)) for i in range(2)]
        self.slots = [Buf(A(f"slot{i}", [128, SLOT_E], BF16)) for i in range(self.NSLOT)]
        self.bfp = Pool([Buf(A(f"bf{i}", [128, 512], BF16)) for i in range(6)])
        self.pring = Ring([Buf(A(f"pt{i}", [128, 512], BF16)) for i in range(4)])
        self.hT = Buf(A("hT", [128, KC, 512], BF16))
        self.big = Buf(A("big", [128, 24, 512], BF16))
        self.fp = Pool([Buf(A(f"f{i}", [128, 544], F32)) for i in range(NFP)])
        self.ident = Buf(A("ident", [128, 128], F32))
        self.ones = Buf(A("ones", [128, 128], BF16))
        self.tri = Buf(A("tri", [128, 128], BF16))
        self.identb = Buf(A("identb", [128, 128], BF16))
        self.onerow = Buf(A("onerow", [1, 128], F32))
        self.invf = Buf(A("invf", [128, 1], F32))
        self.PA = [Buf(A(f"PA{l}", [128, 110], F32)) for l in range(NL)]
        self.PB = [Buf(A(f"PB{l}", [128, 124], F32)) for l in range(NL)]
        self.PC = [Buf(A(f"PC{l}", [128, 66], F32)) for l in range(NL)]
        self.modT = [Buf(A(f"modT{l}", [128, 48, NSEQ], F32)) for l in range(NL)]
        self.cT = Buf(A("cT", [128, NSEQ * 8], F32))
        self.sc1 = Buf(A("sc1", [128, 4, 8], F32))
        self.lamneg = [Buf(A(f"lamneg{l}", [128, 1], F32)) for l in range(NL)]
        self.gsub = [Buf(A(f"gsub{l}", [128, 1], F32)) for l in range(NL)]
        self.gneg = Buf(A("gneg", [128, 4], F32))
        self.halo_sc = Buf(A("halo_sc", [128, 4, 2], F32))
        self.halo_cf = Buf(A("halo_cf", [128, 4, 30], BF16))
        self.halo_ff = Buf(A("halo_ff", [128, FC, 2], F32))
        self.small = Buf(A("small", [128, 8], F32))
        self.ps = [Buf(nc.alloc_psum_tensor(f"ps{i}", [128, 512], F32)) for i in range(8)]
        for pb in self.ps:
            pb.t().excl = True
        self.ps_tiles = {}
        self.psp = Pool(list(self.ps))
        self.psi = 0
        self.sri = 0
        self.ori = 0
        self.ps_last = [0] * 8
        self.psclk = 0

        import os as _os
        STOP = int(_os.environ.get("KSTOP", "99"))
        self.STOP = STOP
        self.consts()
        if STOP <= 1:
            return nc
        for l in range(NL):
            self.small_params(l)
        if STOP <= 2:
            return nc
        self.deferred = []
        self.prep0 = {}
        self.prep0_next = 0
        for l in range(NL):
            self.prep_layer(l)
        self.ensure_prep0(24)
        self.load_x_block(0, 0)
        self.mod_all()
        self.diag_i = 0
        for l in range(NL):
            self.prep_diag(l)
        if STOP <= 4:
            return nc
        self.deferred_all = self.deferred
        self.deferred_done = 0
        if STOP <= 4:
            return nc

        self.sched = []
        for s in range(NSEQ):
            for l in range(NL):
                for b in range(NB):
                    for n in names:
                        self.sched.append((l, self.tidx[n]))
        self.si = 0
        self.loaded = 0

        xloaded = {(0, 0)}
        for s in range(NSEQ):
            for l in range(NL):
                if l >= 1:
                    self.run_deferred(1.0)
                self.layer_seq_init(l, s)
                for b in range(NB):
                    if l == 0 and (s, b) not in xloaded:
                        self.load_x_block(s, b)
                        xloaded.add((s, b))
                    self.mixer_block(l, s, b, skip_norm=(b > 0))

                    if l == 0 and b + 1 < NB and (s, b + 1) not in xloaded:
                        self.load_x_block(s, b + 1)
                        xloaded.add((s, b + 1))
                    if l == NL - 1 and b == NB - 1 and s + 1 < NSEQ:
                        self.load_x_block(s + 1, 0)
                        xloaded.add((s + 1, 0))
                    self.ffn_block(l, s, b, hoist=(b + 1 < NB))
                    if l == NL - 1:
                        self.store_x_block(s, b)
        self._wait(self.eng["sp"], self.out_evs)
        return nc

    def psn(self):
        return self.psp.get()

    def psn4(self):
        return self.psp.get()

    def psf(self, *bs):
        self.psp.free(*bs)

    def consts(self):
        f = self.fp.get()
        ii = f.h[:, 0:128].bitcast(I32)
        self.E("pool", lambda h: h.iota(ii, [[1, 128]], base=0, channel_multiplier=-1), W=[f.t()])
        f2 = self.fp.get()
        self.V("dve", "tensor_copy", R=[f.t()], W=[f2.t()], out=f2.h[:, 0:128], in_=ii)
        self.V("dve", "tensor_single_scalar", R=[f2.t()], W=[self.ident.t()], out=self.ident.h[:], in_=f2.h[:, 0:128], scalar=0.0, op=ALU.is_equal)
        self.V("dve", "tensor_scalar", R=[f2.t()], W=[self.tri.t()], out=self.tri.h[:], in0=f2.h[:, 0:128], scalar1=0.0, scalar2=-30000.0, op0=ALU.is_lt, op1=ALU.mult)
        self.V("dve", "tensor_single_scalar", R=[f2.t()], W=[self.identb.t()], out=self.identb.h[:], in_=f2.h[:, 0:128], scalar=0.0, op=ALU.is_equal)
        self.fp.free(f, f2)
        self.V("pool", "memset", R=[], W=[self.ones.t()], ap=self.ones.h[:], constant=1.0)
        self.V("pool", "memset", R=[], W=[self.onerow.t()], ap=self.onerow.h[:], constant=1.0)
        self.V("pool", "memset", R=[], W=[self.small.t()], ap=self.small.h[:, 0:1], constant=float(NORM_EPS))
        self.V("pool", "memset", R=[], W=[], PW=[self.small.t()], ap=self.small.h[:, 1:2], constant=float(LN_EPS))
        fa = self.fp.get()
        pidx = fa.h[:, 0:1].bitcast(I32)
        self.E("pool", lambda h: h.iota(pidx, [[0, 1]], base=0, channel_multiplier=1), W=[fa.t()])
        fb = self.fp.get()
        self.V("dve", "tensor_copy", R=[fa.t()], W=[fb.t()], out=fb.h[:, 0:1], in_=pidx)
        self.V("dve", "tensor_single_scalar", R=[fb.t()], W=[], PW=[fb.t()], out=fb.h[:, 1:2], in_=fb.h[:, 0:1], scalar=96.0, op=ALU.is_ge)
        self.V("dve", "tensor_single_scalar", R=[fb.t()], W=[], PW=[fb.t()], out=fb.h[:, 2:3], in_=fb.h[:, 0:1], scalar=112.0, op=ALU.is_lt)
        self.V("dve", "tensor_tensor", R=[fb.t()], W=[], PW=[fb.t()], out=fb.h[:, 3:4], in0=fb.h[:, 1:2], in1=fb.h[:, 2:3], op=ALU.mult)
        TWO_PI = float(2.0 * math.pi * (1.0 - 1e-6))
        self.V("dve", "tensor_scalar", R=[fb.t()], W=[], PW=[self.small.t()], out=self.small.h[:, 2:3], in0=fb.h[:, 3:4], scalar1=-2.0 * TWO_PI, scalar2=TWO_PI, op0=ALU.mult, op1=ALU.add)
        self.fp.free(fa, fb)
        f3 = self.fp.get()
        rowv = f3.h[0:1, 0:128].rearrange("o (r i) -> o r i", i=16)
        for i in range(16):
            val = (10000.0 ** (-i / 16.0)) / (2.0 * math.pi)
            self.V("pool", "memset", R=[], W=[f3.t()] if i == 0 else [], PW=[] if i == 0 else [f3.t()], ap=rowv[:, :, i], constant=float(val))
        p = self.psn()
        self.mm(p.h[:, 0:1], f3.h[0:1, 0:128], self.onerow.h[0:1, 0:1], True, True, R=[f3.t(), self.onerow.t()], W=[p.t()])
        self.V("dve", "tensor_copy", R=[p.t()], W=[self.invf.t()], out=self.invf.h[:], in_=p.h[:, 0:1])
        self.psf(p)
        self.fp.free(f3)
        for i in range(2):
            self.V("pool", "memset", R=[], W=[self.VTw[i].t(0), self.VTw[i].t(1)], ap=self.VTw[i].h[:, :, 64:128], constant=1.0)
            self.V("pool", "memset", R=[], W=[self.KTw[i].t("z")], ap=self.KTw[i].h[96:128, :], constant=0.0)
        self.V("pool", "memset", R=[], W=[self.big.t(c) for c in range(24)], ap=self.big.h[:], constant=0.0)

    def epsb(self, eps):
        return self.small.h[:, 0:1] if eps == NORM_EPS else self.small.h[:, 1:2]

    def loadT(self, dst_ap, dst_t, rows, parts):
        st = self.fp.get()
        first = True
        for ap, r0 in parts:
            n = ap.shape[0]
            self.dma("sp", st.h[r0:r0 + n, 0:128], ap, W=[st.t()] if first else [], PW=[] if first else [st.t()])
            first = False
        p = self.psn()
        self.tr(p.h[:, 0:rows], st.h[0:rows, 0:128], self.ident.h[0:rows, 0:rows], R=[st.t(), self.ident.t()], W=[p.t()])
        self.V("dve", "tensor_copy", R=[p.t()], W=[dst_t], out=dst_ap, in_=p.h[:, 0:rows])
        self.psf(p)
        self.fp.free(st)

    def small_params(self, l):
        dr = self.dr
        v8 = lambda n: dr[n][l].rearrange("(c p) -> c p", p=128)
        parts = [(v8("g_pre_mix"), 0), (v8("g_post_mix"), 8), (v8("g_pre_ffn"), 16), (v8("g_post_ffn"), 24),
                 (v8("b_ada"), 32), (v8("da_g_subln"), 80), (v8("mla_g_cq"), 81), (v8("mla_g_ckv"), 84),
                 (dr["sc_conv"][l].rearrange("k (c p) -> (k c) p", p=128), 86),
                 (v8("cf_conv_b"), 98), (v8("cf_ln_g"), 102), (v8("cf_ln_b"), 106)]
        self.loadT(self.PA[l].h[:], self.PA[l].t(), 110, parts)
        self.loadT(self.PB[l].h[:], self.PB[l].t(), 124, [(dr["cf_conv"][l].rearrange("k (c p) -> (k c) p", p=128), 0)])
        self.loadT(self.PC[l].h[:], self.PC[l].t(), 66, [(dr["ffn_conv"][l].rearrange("k (c p) -> (k c) p", p=128), 0)])
        lam_init = 0.8 - 0.6 * math.exp(-0.3 * l)
        self.V("dve", "tensor_scalar", R=[self.PA[l].t()], W=[self.gsub[l].t()], out=self.gsub[l].h[:], in0=self.PA[l].h[:, 80:81],
               scalar1=float(1.0 - lam_init), scalar2=None, op0=ALU.mult)
        st = self.fp.get()
        for i, n in enumerate(["da_lam_q1", "da_lam_k1", "da_lam_q2", "da_lam_k2"]):
            self.dma("sp", st.h[0:1, 64 * i:64 * i + 64], dr[n][l:l + 1, :], W=[st.t()] if i == 0 else [], PW=[] if i == 0 else [st.t()])
        t2, t3, t4, t5 = self.fp.get(), self.fp.get(), self.fp.get(), self.fp.get()
        self.V("dve", "tensor_tensor", R=[st.t()], W=[t2.t()], out=t2.h[0:1, 0:64], in0=st.h[0:1, 0:64], in1=st.h[0:1, 64:128], op=ALU.mult)
        self.V("dve", "tensor_tensor", R=[st.t()], W=[t3.t()], out=t3.h[0:1, 0:64], in0=st.h[0:1, 128:192], in1=st.h[0:1, 192:256], op=ALU.mult)
        self.V("dve", "reduce_sum", R=[t2.t()], W=[t4.t()], out=t4.h[0:1, 0:1], in_=t2.h[0:1, 0:64], axis=AX.X)
        self.V("dve", "reduce_sum", R=[t3.t()], W=[t5.t()], out=t5.h[0:1, 0:1], in_=t3.h[0:1, 0:64], axis=AX.X)
        t6, t7 = self.fp.get(), self.fp.get()
        self.act(t6.h[0:1, 0:1], t4.h[0:1, 0:1], AF.Exp, R=[t4.t()], W=[t6.t()])
        self.act(t7.h[0:1, 0:1], t5.h[0:1, 0:1], AF.Exp, R=[t5.t()], W=[t7.t()])
        self.V("dve", "tensor_tensor", R=[t6.t(), t7.t()], W=[t2.t()], out=t2.h[0:1, 0:1], in0=t6.h[0:1, 0:1], in1=t7.h[0:1, 0:1], op=ALU.subtract)
        p = self.psn()
        self.mm(p.h[:, 0:1], self.onerow.h[0:1, 0:128], t2.h[0:1, 0:1], True, True, R=[t2.t(), self.onerow.t()], W=[p.t()])
        self.V("dve", "tensor_scalar", R=[p.t()], W=[self.lamneg[l].t()], out=self.lamneg[l].h[:], in0=p.h[:, 0:1],
               scalar1=-1.0, scalar2=float(-lam_init), op0=ALU.mult, op1=ALU.add)
        self.psf(p)
        self.fp.free(st, t2, t3, t4, t5, t6, t7)

    def mod_all(self):
        dr, NSEQ = self.dr, self.NSEQ
        n8 = NSEQ * 8
        st = self.fp.get()
        self.dma("sp", st.h[0:n8, 0:128], dr["c"].rearrange("s (c p) -> (s c) p", p=128), W=[st.t()])
        p = self.psn()
        self.tr(p.h[:, 0:n8], st.h[0:n8, 0:128], self.ident.h[0:n8, 0:n8], R=[st.t(), self.ident.t()], W=[p.t()])
        self.act(self.cT.h[:], p.h[:, 0:n8], AF.Silu, R=[p.t()], W=[self.cT.t()])
        self.psf(p)
        self.fp.free(st)
        cv = self.cT.h[:].rearrange("p (s c) -> p s c", c=8)
        k = 0
        for l in range(self.NL):
            firstw = True
            for ct in range(12):
                p = self.psn()
                for gi, (k0, k1) in enumerate([(0, 3), (3, 6), (6, 8)]):
                    nk = k1 - k0
                    sl = self.slots[k % self.NSLOT]
                    k += 1
                    sf = sl.h[:].bitcast(F32)[:, 0:nk * 512].rearrange("p (k n) -> p k n", k=nk)
                    self.dma("sp", sf, dr["w_ada"][l][k0 * 128:k1 * 128, ct * 512:(ct + 1) * 512].rearrange("(k p) n -> p k n", p=128), W=[sl.t()])
                    for kc in range(nk):
                        self.mm(p.h[0:NSEQ, 0:512], cv[:, :, k0 + kc], sf[:, kc, :], k0 + kc == 0, k0 + kc == 7, R=[self.cT.t(), sl.t()], W=[p.t()])
                row = self.fp.get()
                self.V("dve", "tensor_copy", R=[p.t()], W=[row.t()], out=row.h[0:NSEQ, 0:512], in_=p.h[0:NSEQ, 0:512])
                self.psf(p)
                for q in range(4):
                    j = ct * 4 + q
                    p2 = self.psn()
                    self.tr(p2.h[:, 0:NSEQ], row.h[0:NSEQ, q * 128:(q + 1) * 128], self.ident.h[0:NSEQ, 0:NSEQ], R=[row.t(), self.ident.t()], W=[p2.t()])
                    self.V("dve", "tensor_scalar", R=[p2.t(), self.PA[l].t()], W=[self.modT[l].t()] if firstw else [], PW=[] if firstw else [self.modT[l].t()],
                           out=self.modT[l].h[:, j, :], in0=p2.h[:, 0:NSEQ], scalar1=self.PA[l].h[:, 32 + j:33 + j], scalar2=None, op0=ALU.add)
                    self.psf(p2)
                    firstw = False
                self.fp.free(row)

    def seg(self, l, tname, off, src, kc, n):
        ti = self.tidx[tname]
        dst = self.scr[l][ti][:, off:off + kc * n].rearrange("p (k n) -> p k n", k=kc)
        srcv = src.rearrange("(k p) n -> p k n", p=128)
        fn = lambda: self.dma("pool", dst, srcv, PW=[self.scr_t[l][ti]])
        self.reg_prep(l, ti, fn)

    def reg_prep(self, l, ti, fn):
        if l == 0:
            self.prep0.setdefault(ti, []).append(fn)
        else:
            self.deferred.append(fn)

    def ensure_prep0(self, upto):
        while self.prep0_next < min(upto, len(self.tnames)):
            for fn in self.prep0.pop(self.prep0_next, []):
                fn()
            self.prep0_next += 1

    def run_deferred(self, frac):
        n = int(math.ceil(len(self.deferred_all) * frac))
        while self.deferred_done < n:
            self.deferred_all[self.deferred_done]()
            self.deferred_done += 1

    def prep_layer(self, l):
        dr = self.dr
        win = dr["w_in"][l]
        self.seg(l, "qk0a", 0, win[:, OFF_DAQ:OFF_DAQ + 256], 8, 256)
        self.seg(l, "qk0b", 0, win[:, OFF_DAQ + 256:OFF_DAQ + 512], 8, 256)
        self.seg(l, "qk1a", 0, win[:, OFF_DAK:OFF_DAK + 256], 8, 256)
        self.seg(l, "qk1b", 0, win[:, OFF_DAK + 256:OFF_DAK + 512], 8, 256)
        self.seg(l, "va", 0, win[:, OFF_DAV:OFF_DAV + 256], 8, 256)
        self.seg(l, "vb", 0, win[:, OFF_DAV + 256:OFF_DAV + 512], 8, 256)
        self.seg(l, "c", 0, win[:, OFF_CQ:OFF_CQ + 384], 8, 384)
        ti = self.tidx["c2"]
        c2v = self.scr[l][ti][:, 0:8 * 320].rearrange("p (k n) -> p k n", k=8)
        fn_c2 = lambda c2v=c2v, ti=ti: self.dma("pool", c2v[:, :, 0:288], win[:, OFF_CKV:OFF_CKV + 288].rearrange("(k p) n -> p k n", p=128), PW=[self.scr_t[l][ti]])
        self.reg_prep(l, ti, fn_c2)
        self.reg_prep(l, ti, lambda c2v=c2v, ti=ti: self.dma("pool", c2v[:, :, 288:304], win[:, OFF_KR + 16:OFF_KR + 32].rearrange("(k p) n -> p k n", p=128), PW=[self.scr_t[l][ti]]))
        self.reg_prep(l, ti, lambda c2v=c2v, ti=ti: self.dma("pool", c2v[:, :, 304:320], win[:, OFF_KR:OFF_KR + 16].rearrange("(k p) n -> p k n", p=128), PW=[self.scr_t[l][ti]]))
        ti = self.tidx["uq"]
        uqv = self.scr[l][ti][:, 0:3072].rearrange("p (k h n) -> p k h n", k=3, h=8)
        for kc in range(3):
            srcv = dr["mla_w_uq"][l][kc * 128:(kc + 1) * 128, :].rearrange("p (h n) -> p h n", h=8)
            for (d0, d1, s0, s1) in [(0, 96, 0, 96), (96, 112, 80, 96), (112, 128, 64, 80)]:
                self.reg_prep(l, ti, lambda kc=kc, srcv=srcv, d0=d0, d1=d1, s0=s0, s1=s1, ti=ti, uqv=uqv:
                              self.dma("pool", uqv[:, kc, :, d0:d1], srcv[:, :, s0:s1], PW=[self.scr_t[l][ti]]))
        ti = self.tidx["ukv"]
        ukvv = self.scr[l][ti][:, 0:2048].rearrange("p (k t h n) -> p k t h n", k=2, t=2, h=8)
        for kc in range(2):
            srcv = dr["mla_w_ukv"][l][kc * 128:(kc + 1) * 128, :].rearrange("p (h t n) -> p h t n", h=8, t=2)
            for t in range(2):
                self.reg_prep(l, ti, lambda kc=kc, t=t, srcv=srcv, ti=ti, ukvv=ukvv:
                              self.dma("pool", ukvv[:, kc, t, :, :], srcv[:, :, t, :], PW=[self.scr_t[l][ti]]))
        for j in range(4):
            for q, off in enumerate([OFF_SCU, OFF_SCC, OFF_SCB]):
                self.seg(l, f"sc{j}", q * 1024, win[:, off + j * 128:off + (j + 1) * 128], 8, 128)
            self.seg(l, f"cf{j}", 0, win[:, OFF_CFA + j * 128:OFF_CFA + (j + 1) * 128], 8, 128)
            self.seg(l, f"cf{j}", 1024, win[:, OFF_CFG + j * 128:OFF_CFG + (j + 1) * 128], 8, 128)
        for j in range(8):
            for i in range(4):
                tn = f"ma{j}" if i < 2 else f"mb{j}"
                o = OFF_GATE + i * 1024 + j * 128
                self.seg(l, tn, (i % 2) * 1024, win[:, o:o + 128], 8, 128)
            for i, wn in [(0, "da_w_o"), (1, "mla_w_o"), (2, "sc_w_o"), (3, "cf_w_o")]:
                tn = f"ma{j}" if i < 2 else f"mb{j}"
                self.seg(l, tn, 2048 + (i % 2) * 512, dr[wn][l][:, j * 128:(j + 1) * 128], 4, 128)
        for t in range(4):
            self.seg(l, f"mix{t}", 0, dr["w_mix_out"][l][:, t * 256:(t + 1) * 256], 8, 256)
        wup = dr["w_up"][l]
        for c in range(FC):
            self.seg(l, f"up{c}", 0, wup[:, c * 128:(c + 1) * 128], 8, 128)
            self.seg(l, f"up{c}", 1024, wup[:, FFN + c * 128:FFN + (c + 1) * 128], 8, 128)
        for j in range(8):
            self.seg(l, f"dn{j}", 0, dr["w_down"][l][:, j * 128:(j + 1) * 128], FC, 128)

    def prep_diag(self, l):
        for j in range(4):
            for part, (k0, k1) in enumerate([(0, 16), (16, 31)]):
                cb = (self.diag_i % 6) * 4
                self.diag_i += 1
                tiles = [self.big.t(cb + q) for q in range(4)]
                st = self.big.h[:, cb:cb + 4, :].rearrange("p a b -> p (a b)")
                first = True
                for k in range(k0, k1):
                    self.V("dve", "tensor_scalar", R=[self.ident.t(), self.PB[l].t()], W=tiles if first else [], PW=[] if first else tiles,
                           out=st[:, (k - k0) * 128:(k - k0 + 1) * 128], in0=self.ident.h[:], scalar1=self.PB[l].h[:, k * 4 + j:k * 4 + j + 1], scalar2=None, op0=ALU.mult)
                    first = False
                ti = self.tidx[("cda" if part == 0 else "cdb") + str(j)]
                ne = (k1 - k0) * 128
                self.dma("pool", self.scr[l][ti][:, 0:ne], st[:, 0:ne], R=tiles, PW=[self.scr_t[l][ti]])

    def wtile(self, l, name):
        i = self.si
        assert self.sched[i] == (l, self.tidx[name]), (self.sched[i], l, name)
        self.si += 1
        hi = min(len(self.sched), i + self.NSLOT)
        self.ensure_prep0(hi + 16)
        NT = len(self.tnames)
        if self.deferred_done < len(self.deferred_all) and i >= (NT if self.NB > 1 else 0):
            calls_left = max(1, self.NB * NT - i - 2 * self.NSLOT)
            rate = -(-(len(self.deferred_all) - self.deferred_done) // calls_left)
            for _ in range(rate):
                if self.deferred_done < len(self.deferred_all):
                    self.deferred_all[self.deferred_done]()
                    self.deferred_done += 1
        while self.loaded < hi:
            k = self.loaded
            ll, ti = self.sched[k]
            if ll >= 1:
                self.run_deferred(1.0)
            sl = self.slots[k % self.NSLOT]
            ne = self.tsize[ti]
            self.dma("sp", sl.h[:, 0:ne], self.scr[ll][ti][:, 0:ne], R=[self.scr_t[ll][ti]], W=[sl.t()])
            self.loaded += 1
        return self.slots[i % self.NSLOT]

    def layer_seq_init(self, l, s):
        m = self.modT[l]
        PA = self.PA[l]
        R = [m.t(), PA.t()]
        sc = self.sc1
        self.V("dve", "scalar_tensor_tensor", R=R, W=[sc.t()], out=sc.h[:, 0, :], in0=m.h[:, 8:16, s], scalar=1.0, in1=PA.h[:, 0:8], op0=ALU.add, op1=ALU.mult)
        self.V("dve", "tensor_tensor", R=R, W=[], PW=[sc.t()], out=sc.h[:, 1, :], in0=m.h[:, 16:24, s], in1=PA.h[:, 8:16], op=ALU.mult)
        self.V("dve", "scalar_tensor_tensor", R=R, W=[], PW=[sc.t()], out=sc.h[:, 2, :], in0=m.h[:, 32:40, s], scalar=1.0, in1=PA.h[:, 16:24], op0=ALU.add, op1=ALU.mult)
        self.V("dve", "tensor_tensor", R=R, W=[], PW=[sc.t()], out=sc.h[:, 3, :], in0=m.h[:, 40:48, s], in1=PA.h[:, 24:32], op=ALU.mult)
        self.V("pool", "memset", R=[], W=[self.halo_sc.t()], ap=self.halo_sc.h[:], constant=0.0)
        self.V("pool", "memset", R=[], W=[self.halo_cf.t()], ap=self.halo_cf.h[:], constant=0.0)
        self.V("pool", "memset", R=[], W=[self.halo_ff.t()], ap=self.halo_ff.h[:], constant=0.0)

    def load_x_block(self, s, b):
        xT = self.xT
        for tt in range(4):
            t0 = b * TB + tt * 128
            for hf in range(2):
                st = self.fp.get()
                self.dma("act", st.h[:, 0:512], self.dr["x"][s, t0:t0 + 128, hf * 512:(hf + 1) * 512], W=[st.t()])
                p = self.psn()
                for q in range(4):
                    self.mm(p.h[:, q * 128:(q + 1) * 128], st.h[:, q * 128:(q + 1) * 128], self.ident.h[:], True, True, R=[st.t(), self.ident.t()], W=[p.t()])
                self.fp.free(st)
                for q in range(4):
                    c = hf * 4 + q
                    dst = xT.h[:, c, t0:t0 + 128]
                    src = p.h[:, q * 128:(q + 1) * 128]
                    W_, PW_ = ([xT.t((c, b))], []) if tt == 0 else ([], [xT.t((c, b))])
                    if q % 2 == 0:
                        self.E("dve", lambda h, dst=dst, src=src: h.tensor_copy(out=dst, in_=src), R=[p.t()], W=W_, PW=PW_)
                    else:
                        self.E("act", lambda h, dst=dst, src=src: h.activation(out=dst, in_=src, func=AF.Copy), R=[p.t()], W=W_, PW=PW_)
                self.psf(p)

    def store_x_block(self, s, b):
        for tt in range(4):
            t0 = b * TB + tt * 128
            for hf in range(2):
                p = self.psn()
                for q in range(4):
                    c = hf * 4 + q
                    self.mm(p.h[:, q * 128:(q + 1) * 128], self.xT.h[:, c, t0:t0 + 128], self.ident.h[:], True, True, R=[self.xT.t((c, b)), self.ident.t()], W=[p.t()])
                st = self.fp.get()
                self.V("dve", "tensor_copy", R=[p.t()], W=[st.t()], out=st.h[:, 0:512], in_=p.h[:, 0:512])
                self.psf(p)
                ev = self.dma("pool", self.out[s, t0:t0 + 128, hf * 512:(hf + 1) * 512], st.h[:, 0:512], R=[st.t()])
                self.out_evs.append(ev)
                self.fp.free(st)

    def rstd_from_ss(self, ps_ss, n, eps):
        r = self.fp.get()
        self.act(r.h[:, 0:512], ps_ss.h[:, 0:512], AF.Ln, R=[ps_ss.t(), self.small.t()], W=[r.t()], scale=float(1.0 / n), bias=self.epsb(eps))
        self.act(r.h[:, 0:512], r.h[:, 0:512], AF.Exp, R=[r.t()], W=[r.t()], scale=-0.5)
        return r

    def norm_in(self, l, s, b, which):
        xT, hT = self.xT, self.hT
        c0 = b * TB
        ss = self.psn()
        for c in range(KC):
            sq = self.bfp.get()
            self.act(sq.h[:], xT.h[:, c, c0:c0 + TB], AF.Square, R=[xT.t((c, b))], W=[sq.t()])
            self.mm(ss.h[:], self.ones.h[:], sq.h[:], c == 0, c == KC - 1, R=[self.ones.t(), sq.t()], W=[ss.t()])
            self.bfp.free(sq)
        rs = self.rstd_from_ss(ss, D, NORM_EPS)
        self.psf(ss)
        m = self.modT[l]
        for c in range(KC):
            t = self.fp.get()
            self.V("dve", "tensor_tensor", R=[xT.t((c, b)), rs.t()], W=[t.t()], out=t.h[:, 0:512], in0=xT.h[:, c, c0:c0 + TB], in1=rs.h[:, 0:512], op=ALU.mult)
            self.act(hT.h[:, c, :], t.h[:, 0:512], AF.Identity, R=[t.t(), self.sc1.t(), m.t()], W=[hT.t(c)],
                     scale=self.sc1.h[:, 2 * which, c:c + 1], bias=m.h[:, (24 * which) + c, s:s + 1])
            self.fp.free(t)
        self.fp.free(rs)

    def post_norm_residual(self, l, s, b, which, pss):
        xT = self.xT
        c0 = b * TB
        ys = []
        ss = self.psn() if len(self.psp.fl) > 0 else None
        own_ss = ss is not None
        for c in range(KC):
            sq = self.bfp.get()
            self.act(sq.h[:], pss[c].h[:], AF.Square, R=[pss[c].t()], W=[sq.t()])
            y = self.fp.get()
            if c >= NPOOLC:
                self.act(y.h[:, 0:512], pss[c].h[:], AF.Identity, R=[pss[c].t(), self.sc1.t()], W=[y.t()], scale=self.sc1.h[:, 2 * which + 1, c:c + 1])
            else:
                self.V("dve", "tensor_copy", R=[pss[c].t()], W=[y.t()], out=y.h[:, 0:512], in_=pss[c].h[:])
            ys.append(y)
            if ss is None:
                ss = pss[0]
            else:
                if c > 0 or own_ss:
                    pass
            self.mm(ss.h[:], self.ones.h[:], sq.h[:], c == 0, c == KC - 1, R=[self.ones.t(), sq.t()], W=[ss.t()])
            self.bfp.free(sq)
            if not (pss[c] is ss):
                self.psf(pss[c])
        rs = self.rstd_from_ss(ss, D, NORM_EPS)
        self.psf(ss)
        for c in range(NPOOLC, KC):
            y = ys[c]
            self.V("pool", "tensor_tensor", R=[y.t(), rs.t()], W=[y.t()], out=y.h[:, 0:512], in0=y.h[:, 0:512], in1=rs.h[:, 0:512], op=ALU.mult)
            self.V("pool", "tensor_tensor", R=[y.t(), xT.t((c, b))], W=[xT.t((c, b))], out=xT.h[:, c, c0:c0 + TB], in0=y.h[:, 0:512], in1=xT.h[:, c, c0:c0 + TB], op=ALU.add)
        for c in range(NPOOLC):
            y = ys[c]
            self.V("dve", "tensor_tensor", R=[y.t(), rs.t()], W=[y.t()], out=y.h[:, 0:512], in0=y.h[:, 0:512], in1=rs.h[:, 0:512], op=ALU.mult)
            self.V("dve", "scalar_tensor_tensor", R=[y.t(), self.sc1.t(), xT.t((c, b))], W=[xT.t((c, b))], out=xT.h[:, c, c0:c0 + TB],
                   in0=y.h[:, 0:512], scalar=self.sc1.h[:, 2 * which + 1, c:c + 1], in1=xT.h[:, c, c0:c0 + TB], op0=ALU.mult, op1=ALU.add)
        self.fp.free(rs, *ys)

    def rope_tables(self, s, b):
        c0 = b * TB
        pi_ = self.fp.get()
        piv = pi_.h[:, 0:512].bitcast(I32)
        self.dma("sp", piv, self.dr["positions"][s:s + 1, c0:c0 + TB].broadcast_to([128, TB]), W=[pi_.t()])
        y = self.fp.get()
        self.V("dve", "tensor_copy", R=[pi_.t()], W=[y.t()], out=y.h[:, 0:512], in_=piv)
        self.V("dve", "tensor_scalar", R=[y.t(), self.invf.t()], W=[y.t()], out=y.h[:, 0:512], in0=y.h[:, 0:512], scalar1=self.invf.h[:, 0:1], scalar2=None, op0=ALU.mult)
        self.fp.free(pi_)
        outs = []
        for shift in (0.25, 0.0):
            z = self.fp.get()
            self.V("dve", "tensor_scalar", R=[y.t()], W=[z.t()], out=z.h[:, 0:512], in0=y.h[:, 0:512], scalar1=float(shift), scalar2=None, op0=ALU.add)
            zi = self.fp.get()
            ziv = zi.h[:, 0:512].bitcast(I32)
            self.V("dve", "tensor_copy", R=[z.t()], W=[zi.t()], out=ziv, in_=z.h[:, 0:512])
            zf = self.fp.get()
            self.V("dve", "tensor_copy", R=[zi.t()], W=[zf.t()], out=zf.h[:, 0:512], in_=ziv)
            self.V("dve", "tensor_tensor", R=[z.t(), zf.t()], W=[z.t()], out=z.h[:, 0:512], in0=z.h[:, 0:512], in1=zf.h[:, 0:512], op=ALU.subtract)
            self.V("dve", "tensor_single_scalar", R=[z.t()], W=[zi.t()], out=zi.h[:, 0:512], in_=z.h[:, 0:512], scalar=0.5, op=ALU.is_gt)
            self.V("dve", "tensor_single_scalar", R=[z.t()], W=[zf.t()], out=zf.h[:, 0:512], in_=z.h[:, 0:512], scalar=-0.5, op=ALU.is_lt)
            self.V("dve", "tensor_tensor", R=[z.t(), zi.t()], W=[z.t()], out=z.h[:, 0:512], in0=z.h[:, 0:512], in1=zi.h[:, 0:512], op=ALU.subtract)
            self.V("dve", "tensor_tensor", R=[z.t(), zf.t()], W=[z.t()], out=z.h[:, 0:512], in0=z.h[:, 0:512], in1=zf.h[:, 0:512], op=ALU.add)
            o = self.fp.get()
            if shift == 0.0:
                self.act(o.h[:, 0:512], z.h[:, 0:512], AF.Sin, R=[z.t(), self.small.t()], W=[o.t()], scale=self.small.h[:, 2:3])
            else:
                self.act(o.h[:, 0:512], z.h[:, 0:512], AF.Sin, R=[z.t()], W=[o.t()], scale=float(2.0 * math.pi * (1.0 - 1e-6)))
            self.fp.free(z, zi, zf)
            outs.append(o)
        self.fp.free(y)
        return outs[0], outs[1]

    def attn_stream(self, nk, qb, kt_fn, q_ap, q_t, scale, pv_fn, R_k):
        LA = 3
        ring = [self.psn() for _ in range(LA + 1)]
        pend = {}
        cnt = [0]

        def qk(k):
            q0 = 128 * max(0, k - 4 * qb)
            p = ring[cnt[0] % (LA + 1)]
            cnt[0] += 1
            diag = k >= 4 * qb
            self.mm(p.h[:, q0:512], kt_fn(k), q_ap(q0), True, not diag, R=R_k + [q_t], W=[p.t()])
            if diag:
                self.mm(p.h[:, q0:q0 + 128], self.identb.h[:], self.tri.h[:], False, True, R=[self.identb.t(), self.tri.t()], W=[p.t()])
            pend[k] = (p, q0)
        for k in range(min(LA, nk)):
            qk(k)
        for k in range(nk):
            p, q0 = pend.pop(k)
            pt = self.pring.get()
            self.act(pt.h[:, q0:512], p.h[:, q0:512], AF.Exp, R=[p.t()], W=[pt.t()], scale=float(scale))
            if k + LA < nk:
                qk(k + LA)
            pv_fn(k, pt, q0, k == 0, k == nk - 1)
        self.psf(*ring)

    def mixer_block(self, l, s, b, skip_norm=False):
        NKB = 4 * (b + 1)
        c0 = b * TB
        hT, big = self.hT, self.big
        CQ, QM, QD = 8, 11, 19
        if not skip_norm:
            self.norm_in(l, s, b, 0)
        hR = [hT.t(c) for c in range(KC)]
        cs, sn = self.rope_tables(s, b)
        QZ = 4
        for h in range(4):
            for m in range(2):
                c = QZ + 2 * h + m
                zr = slice((1 - m) * 64, (1 - m) * 64 + 64)
                self.V("pool", "memset", R=[], W=[big.t(c)], ap=big.h[zr, c, :], constant=0.0)

        pv4 = [self.psn() for _ in range(4)]
        for half, nm in enumerate(["va", "vb"]):
            w = self.wtile(l, nm)
            wv = w.h[:, 0:2048].rearrange("p (k n) -> p k n", k=8)
            for tt in range(4):
                p = pv4[tt]
                for kc in range(8):
                    self.mm(p.h[:, half * 256:(half + 1) * 256], hT.h[:, kc, tt * 128:(tt + 1) * 128], wv[:, kc, :], kc == 0, kc == 7, R=[w.t()] + hR, W=[p.t()])
        for tt in range(4):
            kb = b * 4 + tt
            if tt % 2 == 0:
                self.act(self.Vda.h[:, kb, :], pv4[tt].h[:], AF.Copy, R=[pv4[tt].t()], W=[self.Vda.t(kb)])
            else:
                self.V("dve", "tensor_copy", R=[pv4[tt].t()], W=[self.Vda.t(kb)], out=self.Vda.h[:, kb, :], in_=pv4[tt].h[:])
        self.psf(*pv4)
        for half, nm in enumerate(["qk0a", "qk0b"]):
            w = self.wtile(l, nm)
            wv = w.h[:, 0:2048].rearrange("p (k n) -> p k n", k=8)
            for hh in range(2):
                h = half * 2 + hh
                p = self.psn()
                for kc in range(8):
                    self.mm(p.h[:], wv[:, kc, hh * 128:(hh + 1) * 128], hT.h[:, kc, :], kc == 0, kc == 7, R=[w.t()] + hR, W=[p.t()])
                self.act(big.h[0:64, QZ + 2 * h, :], p.h[0:64, :], AF.Copy, R=[p.t(), big.t(QZ + 2 * h)], W=[], PW=[big.t(QZ + 2 * h)])
                self.V("dve", "tensor_copy", R=[p.t(), big.t(QZ + 2 * h + 1)], W=[], PW=[big.t(QZ + 2 * h + 1)], out=big.h[64:128, QZ + 2 * h + 1, :], in_=p.h[64:128, :])
                self.psf(p)
        for half, nm in enumerate(["qk1a", "qk1b"]):
            w = self.wtile(l, nm)
            wv = w.h[:, 0:2048].rearrange("p (k n) -> p k n", k=8)
            for hh in range(2):
                h = half * 2 + hh
                p = self.psn()
                for kc in range(8):
                    self.mm(p.h[:], wv[:, kc, hh * 128:(hh + 1) * 128], hT.h[:, kc, :], kc == 0, kc == 7, R=[w.t()] + hR, W=[p.t()])
                self.V("dve", "tensor_copy", R=[p.t()], W=[self.KTda.t((h, b))], out=self.KTda.h[:, h, c0:c0 + TB], in_=p.h[:])
                self.psf(p)
        def da_finish(h, a):
            sq = self.bfp.get()
            self.act(sq.h[:], a.h[:, 0:512], AF.Square, R=[a.t()], W=[sq.t()])
            ss = self.psn()
            self.mm(ss.h[:], self.ones.h[:], sq.h[:], True, True, R=[self.ones.t(), sq.t()], W=[ss.t()])
            self.bfp.free(sq)
            rs = self.rstd_from_ss(ss, 128, NORM_EPS)
            self.psf(ss)
            self.V("dve", "scalar_tensor_tensor", R=[a.t(), rs.t(), self.gsub[l].t()], W=[big.t(h)], out=big.h[:, h, :], in0=a.h[:, 0:512],
                   scalar=self.gsub[l].h[:, 0:1], in1=rs.h[:, 0:512], op0=ALU.mult, op1=ALU.mult)
            self.fp.free(a, rs)
        pending = None
        for h in range(4):
            tms = []
            for m in range(2):
                rows = slice(m * 64, m * 64 + 64)
                O, L = self.psn(), self.psn()

                def pv(k, pt, q0, first, last, O=O, L=L, h=h):
                    self.mm(O.h[:, q0:512], self.Vda.h[:, k, h * 128:(h + 1) * 128], pt.h[:, q0:512], first, last, R=[self.Vda.t(k), pt.t()], W=[O.t()])
                    self.mm(L.h[:, q0:512], self.ones.h[:], pt.h[:, q0:512], first, last, R=[self.ones.t(), pt.t()], W=[L.t()])
                self.attn_stream(NKB, b,
                                 lambda k, h=h: self.KTda.h[:, h, k * 128:(k + 1) * 128],
                                 lambda q0, h=h, m=m: big.h[:, QZ + 2 * h + m, q0:512], big.t(QZ + 2 * h + m), 0.125, pv,
                                 R_k=[self.KTda.t((h, bb)) for bb in range(b + 1)])
                if m == 0 and pending is not None:
                    da_finish(*pending)
                    pending = None
                r = self.fp.get()
                self.act(r.h[:, 0:512], L.h[:], AF.Ln, R=[L.t()], W=[r.t()])
                self.act(r.h[:, 0:512], r.h[:, 0:512], AF.Exp, R=[r.t()], W=[r.t()], scale=-1.0)
                tm = self.fp.get()
                self.V("dve", "tensor_tensor", R=[O.t(), r.t()], W=[tm.t()], out=tm.h[:, 0:512], in0=O.h[:], in1=r.h[:, 0:512], op=ALU.mult)
                self.fp.free(r)
                self.psf(O, L)
                tms.append(tm)
            a = self.fp.get()
            self.V("dve", "scalar_tensor_tensor", R=[tms[0].t(), tms[1].t(), self.lamneg[l].t()], W=[a.t()], out=a.h[:, 0:512], in0=tms[1].h[:, 0:512],
                   scalar=self.lamneg[l].h[:, 0:1], in1=tms[0].h[:, 0:512], op0=ALU.mult, op1=ALU.add)
            self.fp.free(*tms)
            pending = (h, a)

        w = self.wtile(l, "c")
        wv = w.h[:, 0:3072].rearrange("p (k n) -> p k n", k=8)
        ssq = self.psn()
        pcq = []
        for j in range(3):
            p = self.psn()
            for kc in range(8):
                self.mm(p.h[:], wv[:, kc, j * 128:(j + 1) * 128], hT.h[:, kc, :], kc == 0, kc == 7, R=[w.t()] + hR, W=[p.t()])
            if j == 0 and pending is not None:
                da_finish(*pending)
                pending = None
            sq = self.bfp.get()
            self.act(sq.h[:], p.h[:], AF.Square, R=[p.t()], W=[sq.t()])
            pcq.append(p)
            self.mm(ssq.h[:], self.ones.h[:], sq.h[:], j == 0, j == 2, R=[self.ones.t(), sq.t()], W=[ssq.t()])
            self.bfp.free(sq)
        rq = self.rstd_from_ss(ssq, 384, NORM_EPS)
        self.psf(ssq)
        for j in range(3):
            self.V("dve", "scalar_tensor_tensor", R=[pcq[j].t(), rq.t(), self.PA[l].t()], W=[big.t(CQ + j)], out=big.h[:, CQ + j, :], in0=pcq[j].h[:],
                   scalar=self.PA[l].h[:, 81 + j:82 + j], in1=rq.h[:, 0:512], op0=ALU.mult, op1=ALU.mult)
            self.psf(pcq[j])
        self.fp.free(rq)
        w2 = self.wtile(l, "c2")
        w2v = w2.h[:, 0:2560].rearrange("p (k n) -> p k n", k=8)
        pk = [self.psn(), self.psn()]
        for j in range(2):
            for kc in range(8):
                self.mm(pk[j].h[:], w2v[:, kc, j * 128:(j + 1) * 128], hT.h[:, kc, :], kc == 0, kc == 7, R=[w2.t()] + hR, W=[pk[j].t()])
        pr = self.psn()
        for kc in range(8):
            self.mm(pr.h[64:128, :], w2v[:, kc, 256:320], hT.h[:, kc, :], kc == 0, kc == 7, R=[w2.t()] + hR, W=[pr.t()])
        ssk = self.psn()
        for j in range(2):
            sq = self.bfp.get()
            self.act(sq.h[:], pk[j].h[:], AF.Square, R=[pk[j].t()], W=[sq.t()])
            self.mm(ssk.h[:], self.ones.h[:], sq.h[:], j == 0, j == 1, R=[self.ones.t(), sq.t()], W=[ssk.t()])
            self.bfp.free(sq)
        rk = self.rstd_from_ss(ssk, 256, NORM_EPS)
        self.psf(ssk)
        for j in range(2):
            self.V("dve", "scalar_tensor_tensor", R=[pk[j].t(), rk.t(), self.PA[l].t()], W=[self.lat.t((j, b))], out=self.lat.h[:, j, c0:c0 + TB], in0=pk[j].h[:],
                   scalar=self.PA[l].h[:, 84 + j:85 + j], in1=rk.h[:, 0:512], op0=ALU.mult, op1=ALU.mult)
        self.psf(*pk)
        self.fp.free(rk)
        t1, t2 = self.fp.get(), self.fp.get()
        self.V("dve", "tensor_tensor", R=[pr.t(), sn.t()], W=[t1.t()], out=t1.h[64:96, 0:512], in0=pr.h[96:128, :], in1=sn.h[96:128, 0:512], op=ALU.mult)
        self.V("dve", "tensor_tensor", R=[pr.t(), cs.t()], W=[t2.t()], out=t2.h[64:96, 0:512], in0=pr.h[64:96, :], in1=cs.h[64:96, 0:512], op=ALU.mult)
        self.psf(pr)
        for i in range(2):
            self.V("pool", "tensor_tensor", R=[t1.t(), t2.t()], W=[self.KTw[i].t(("r", b))], out=self.KTw[i].h[64:96, c0:c0 + TB], in0=t1.h[64:96, 0:512], in1=t2.h[64:96, 0:512], op=ALU.add)
        self.fp.free(t1, t2)
        wq = self.wtile(l, "uq")
        wqv = wq.h[:, 0:3072].rearrange("p (k h n) -> p k h n", k=3, h=8)
        for h in range(8):
            p = self.psn()
            for kc in range(3):
                self.mm(p.h[:], wqv[:, kc, h, :], big.h[:, CQ + kc, :], kc == 0, kc == 2, R=[wq.t(), big.t(CQ + kc)], W=[p.t()])
            qt = big.t(QM + h)
            self.act(big.h[0:64, QM + h, :], p.h[0:64, :], AF.Copy, R=[p.t()], W=[qt])
            u1, u2 = self.fp.get(), self.fp.get()
            self.V("dve", "tensor_tensor", R=[p.t(), sn.t()], W=[u1.t()], out=u1.h[64:96, 0:512], in0=p.h[96:128, :], in1=sn.h[96:128, 0:512], op=ALU.mult)
            self.V("dve", "tensor_tensor", R=[p.t(), cs.t()], W=[u2.t()], out=u2.h[64:96, 0:512], in0=p.h[64:96, :], in1=cs.h[64:96, 0:512], op=ALU.mult)
            self.psf(p)
            self.V("pool", "tensor_tensor", R=[u1.t(), u2.t()], W=[], PW=[qt], out=big.h[64:96, QM + h, :], in0=u1.h[64:96, 0:512], in1=u2.h[64:96, 0:512], op=ALU.add)
            self.fp.free(u1, u2)
        self.fp.free(cs, sn)

        wk = self.wtile(l, "ukv")
        wkv = wk.h[:, 0:2048].rearrange("p (k t h n) -> p k t h n", k=2, t=2, h=8)
        latR = [self.lat.t((j, bb)) for j in range(2) for bb in range(b + 1)]

        def recompute(h):
            KT, VT = self.KTw[h % 2], self.VTw[h % 2]
            for kg in range(b + 1):
                p = self.psn()
                for kc in range(2):
                    self.mm(p.h[0:64, :], wkv[:, kc, 0, h, :], self.lat.h[:, kc, kg * TB:(kg + 1) * TB], kc == 0, kc == 1, R=[wk.t()] + latR, W=[p.t()])
                self.act(KT.h[0:64, kg * TB:(kg + 1) * TB], p.h[0:64, :], AF.Copy, R=[p.t()], W=[KT.t(("n", kg))])
                self.psf(p)
            for g in range((NKB + 7) // 8):
                p = self.psn()
                nk = min(8, NKB - g * 8)
                for q in range(nk):
                    kb = g * 8 + q
                    for kc in range(2):
                        self.mm(p.h[:, q * 64:(q + 1) * 64], self.lat.h[:, kc, kb * 128:(kb + 1) * 128], wkv[:, kc, 1, h, :], kc == 0, kc == 1,
                                R=[wk.t()] + latR, W=[p.t()])
                self.V("dve", "tensor_copy", R=[p.t()], W=[VT.t(g)], out=VT.h[:, g * 8:g * 8 + nk, 0:64],
                       in_=p.h[:, 0:nk * 64].rearrange("p (q n) -> p q n", n=64))
                self.psf(p)
        recompute(0)
        for h in range(8):
            if h + 1 < 8:
                recompute(h + 1)
            KT, VT = self.KTw[h % 2], self.VTw[h % 2]
            O = self.psn()

            def pv(k, pt, q0, first, last, O=O, VT=VT):
                self.mm(O.h[:, q0:512], VT.h[:, k, :], pt.h[:, q0:512], first, last, R=[VT.t(k // 8), pt.t()], W=[O.t()])
            self.attn_stream(NKB, b,
                             lambda k, KT=KT: KT.h[:, k * 128:(k + 1) * 128],
                             lambda q0, h=h: big.h[:, QM + h, q0:512], big.t(QM + h), 96.0 ** -0.5, pv,
                             R_k=[KT.t("z")] + [KT.t(("n", kg)) for kg in range(b + 1)] + [KT.t(("r", bb)) for bb in range(b + 1)])
            r = self.fp.get()
            self.act(r.h[0:64, 0:512], O.h[64:128, :], AF.Ln, R=[O.t()], W=[r.t()])
            self.act(r.h[0:64, 0:512], r.h[0:64, 0:512], AF.Exp, R=[r.t()], W=[r.t()], scale=-1.0)
            ro = (h % 2) * 64
            bt = big.t(4 + h // 2)
            self.V("dve", "tensor_tensor", R=[O.t(), r.t()], W=[bt] if h % 2 == 0 else [], PW=[] if h % 2 == 0 else [bt],
                   out=big.h[ro:ro + 64, 4 + h // 2, :], in0=O.h[0:64, :], in1=r.h[0:64, 0:512], op=ALU.mult)
            self.psf(O)
            self.fp.free(r)

        PA = self.PA[l]
        for j in range(4):
            w = self.wtile(l, f"sc{j}")
            wv = w.h[:, 0:3072].rearrange("p (q k n) -> p q k n", q=3, k=8)
            pp = []
            for q in range(3):
                p = self.psn()
                for kc in range(8):
                    self.mm(p.h[:], wv[:, q, kc, :], hT.h[:, kc, :], kc == 0, kc == 7, R=[w.t()] + hR, W=[p.t()])
                pp.append(p)
            us = self.fp.get()
            self.act(us.h[:, 0:512], pp[0].h[:], AF.Copy, R=[pp[0].t()], W=[us.t()])
            ub = self.fp.get()
            self.V("pool", "tensor_copy", R=[self.halo_sc.t()], W=[ub.t()], out=ub.h[:, 0:2], in_=self.halo_sc.h[:, j, :])
            self.V("dve", "tensor_tensor", R=[pp[1].t(), us.t()], W=[], PW=[ub.t()], out=ub.h[:, 2:514], in0=pp[1].h[:], in1=us.h[:, 0:512], op=ALU.mult)
            self.fp.free(us)
            acc = self.conv_taps(ub, 3, lambda k, j=j: PA.h[:, 86 + k * 4 + j:87 + k * 4 + j], [PA.t()], None, "dve")
            self.V("pool", "tensor_copy", R=[ub.t()], W=[], PW=[self.halo_sc.t()], out=self.halo_sc.h[:, j, :], in_=ub.h[:, 512:514])
            self.V("dve", "tensor_tensor", R=[pp[2].t(), acc.t()], W=[big.t(8 + j)], out=big.h[:, 8 + j, :], in0=pp[2].h[:], in1=acc.h[:, 0:512], op=ALU.mult)
            self.psf(*pp)
            self.fp.free(ub, acc)

        PB = self.PB[l]
        vs = []
        ubs = []
        for j in range(4):
            w = self.wtile(l, f"cf{j}")
            wv = w.h[:, 0:2048].rearrange("p (q k n) -> p q k n", q=2, k=8)
            pa, pg = self.psn(), self.psn()
            for kc in range(8):
                self.mm(pa.h[:], wv[:, 0, kc, :], hT.h[:, kc, :], kc == 0, kc == 7, R=[w.t()] + hR, W=[pa.t()])
            for kc in range(8):
                self.mm(pg.h[:], wv[:, 1, kc, :], hT.h[:, kc, :], kc == 0, kc == 7, R=[w.t()] + hR, W=[pg.t()])
            sg = self.fp.get()
            self.act(sg.h[:, 0:512], pg.h[:], AF.Sigmoid, R=[pg.t()], W=[sg.t()])
            ubf = self.fp.get()
            ub = ubf.h[:, 0:272].bitcast(BF16)
            self.V("pool", "tensor_copy", R=[self.halo_cf.t()], W=[ubf.t()], out=ub[:, 0:30], in_=self.halo_cf.h[:, j, :])
            self.V("dve", "tensor_tensor", R=[pa.t(), sg.t()], W=[], PW=[ubf.t()], out=ub[:, 30:542], in0=pa.h[:], in1=sg.h[:, 0:512], op=ALU.mult)
            self.psf(pa, pg)
            self.fp.free(sg)
            self.V("pool", "tensor_copy", R=[ubf.t()], W=[], PW=[self.halo_cf.t()], out=self.halo_cf.h[:, j, :], in_=ub[:, 512:542])
            ubs.append((ubf, ub))
        for j in range(4):
            ubf, ub = ubs[j]
            acc = self.psn()
            wa = self.wtile(l, f"cda{j}")
            for k in range(16):
                self.mm(acc.h[:], wa.h[:, k * 128:(k + 1) * 128], ub[:, k:k + 512], k == 0, False, R=[wa.t(), ubf.t()], W=[acc.t()])
            wb = self.wtile(l, f"cdb{j}")
            for k in range(16, 31):
                self.mm(acc.h[:], wb.h[:, (k - 16) * 128:(k - 15) * 128], ub[:, k:k + 512], False, k == 30, R=[wb.t(), ubf.t()], W=[acc.t()])
            v = self.fp.get()
            self.V("dve", "tensor_scalar", R=[acc.t(), PA.t()], W=[v.t()], out=v.h[:, 0:512], in0=acc.h[:], scalar1=PA.h[:, 98 + j:99 + j], scalar2=None, op0=ALU.add)
            self.psf(acc)
            self.fp.free(ubf)
            vs.append(v)
        s1, s2 = self.psn(), self.psn()
        for j in range(4):
            vb = self.bfp.get()
            self.act(vb.h[:], vs[j].h[:, 0:512], AF.Copy, R=[vs[j].t()], W=[vb.t()])
            sq = self.bfp.get()
            self.act(sq.h[:], vs[j].h[:, 0:512], AF.Square, R=[vs[j].t()], W=[sq.t()])
            self.mm(s1.h[:], self.ones.h[:], vb.h[:], j == 0, j == 3, R=[self.ones.t(), vb.t()], W=[s1.t()])
            self.mm(s2.h[:], self.ones.h[:], sq.h[:], j == 0, j == 3, R=[self.ones.t(), sq.t()], W=[s2.t()])
            self.bfp.free(vb, sq)
        mean, msq = self.fp.get(), self.fp.get()
        self.V("dve", "tensor_scalar", R=[s1.t()], W=[mean.t()], out=mean.h[:, 0:512], in0=s1.h[:], scalar1=float(1.0 / 512), scalar2=None, op0=ALU.mult)
        self.V("dve", "tensor_tensor", R=[mean.t()], W=[msq.t()], out=msq.h[:, 0:512], in0=mean.h[:, 0:512], in1=mean.h[:, 0:512], op=ALU.mult)
        var = self.fp.get()
        self.V("dve", "scalar_tensor_tensor", R=[s2.t(), msq.t()], W=[var.t()], out=var.h[:, 0:512], in0=s2.h[:], scalar=float(1.0 / 512), in1=msq.h[:, 0:512], op0=ALU.mult, op1=ALU.subtract)
        self.psf(s1, s2)
        self.act(msq.h[:, 0:512], var.h[:, 0:512], AF.Ln, R=[var.t(), self.small.t()], W=[msq.t()], bias=self.epsb(LN_EPS), scale=1.0)
        self.act(msq.h[:, 0:512], msq.h[:, 0:512], AF.Exp, R=[msq.t()], W=[msq.t()], scale=-0.5)
        rstd = msq
        self.fp.free(var)
        for j in range(4):
            v = vs[j]
            self.V("dve", "tensor_tensor", R=[v.t(), mean.t()], W=[v.t()], out=v.h[:, 0:512], in0=v.h[:, 0:512], in1=mean.h[:, 0:512], op=ALU.subtract)
            self.V("dve", "tensor_tensor", R=[v.t(), rstd.t()], W=[v.t()], out=v.h[:, 0:512], in0=v.h[:, 0:512], in1=rstd.h[:, 0:512], op=ALU.mult)
            self.act(big.h[:, 12 + j, :], v.h[:, 0:512], AF.Silu, R=[v.t(), PA.t()], W=[big.t(12 + j)], scale=PA.h[:, 102 + j:103 + j], bias=PA.h[:, 106 + j:107 + j])
        self.fp.free(mean, rstd, *vs)

        for j in range(8):
            m = None
            for half, nm in enumerate([f"ma{j}", f"mb{j}"]):
                w = self.wtile(l, nm)
                gv = w.h[:, 0:2048].rearrange("p (i k n) -> p i k n", i=2, k=8)
                bv = w.h[:, 2048:3072].rearrange("p (i k n) -> p i k n", i=2, k=4)
                for ii in range(2):
                    i = half * 2 + ii
                    py, pg = self.psn(), self.psn()
                    for kc in range(8):
                        self.mm(pg.h[:], gv[:, ii, kc, :], hT.h[:, kc, :], kc == 0, kc == 7, R=[w.t()] + hR, W=[pg.t()])
                    for kc in range(4):
                        self.mm(py.h[:], bv[:, ii, kc, :], big.h[:, 4 * i + kc, :], kc == 0, kc == 3, R=[w.t(), big.t(4 * i + kc)], W=[py.t()])
                    sg = self.fp.get()
                    self.act(sg.h[:, 0:512], pg.h[:], AF.Sigmoid, R=[pg.t()], W=[sg.t()])
                    t = self.fp.get()
                    self.V("dve", "tensor_tensor", R=[py.t(), sg.t()], W=[t.t()], out=t.h[:, 0:512], in0=py.h[:], in1=sg.h[:, 0:512], op=ALU.mult)
                    self.psf(py, pg)
                    self.fp.free(sg)
                    if m is None:
                        m = t
                    elif i < 3:
                        self.V("pool", "tensor_tensor", R=[m.t(), t.t()], W=[t.t()], out=t.h[:, 0:512], in0=m.h[:, 0:512], in1=t.h[:, 0:512], op=ALU.add)
                        self.fp.free(m)
                        m = t
                    else:
                        self.V("pool", "tensor_tensor", R=[m.t(), t.t()], W=[big.t(16 + j)], out=big.h[:, 16 + j, :], in0=m.h[:, 0:512], in1=t.h[:, 0:512], op=ALU.add)
                        self.fp.free(m, t)

        pss = []
        for t in range(4):
            w = self.wtile(l, f"mix{t}")
            wv = w.h[:, 0:2048].rearrange("p (k n) -> p k n", k=8)
            for q in range(2):
                p = self.psn()
                for kc in range(8):
                    self.mm(p.h[:], wv[:, kc, q * 128:(q + 1) * 128], big.h[:, 16 + kc, :], kc == 0, kc == 7, R=[w.t(), big.t(16 + kc)], W=[p.t()])
                pss.append(p)
        self.post_norm_residual(l, s, b, 0, pss)

    def conv_taps(self, ub, K, wfn, Rw, bias, en):
        acc = self.fp.get()
        k = K - 1
        if bias is None:
            self.V(en, "tensor_scalar", R=[ub.t()] + Rw, W=[acc.t()], out=acc.h[:, 0:512], in0=ub.h[:, k:k + 512], scalar1=wfn(k), scalar2=None, op0=ALU.mult)
        else:
            self.V(en, "tensor_scalar", R=[ub.t()] + Rw, W=[acc.t()], out=acc.h[:, 0:512], in0=ub.h[:, k:k + 512], scalar1=wfn(k), scalar2=bias, op0=ALU.mult, op1=ALU.add)
        for k in range(K - 2, -1, -1):
            self.V(en, "scalar_tensor_tensor", R=[ub.t(), acc.t()] + Rw, W=[acc.t()], out=acc.h[:, 0:512], in0=ub.h[:, k:k + 512], scalar=wfn(k), in1=acc.h[:, 0:512], op0=ALU.mult, op1=ALU.add)
        return acc

    def ffn_gate_mul(self, g, pb, sl, c):
        self.V("dve", "tensor_tensor", R=[pb.t(), sl.t()], W=[g.t(c)], out=g.h[:, c, :], in0=pb.h[:], in1=sl.h[:, 0:512], op=ALU.mult)
        self.psf(pb)
        self.fp.free(sl)

    def ffn_block(self, l, s, b, hoist=False):
        hT = self.hT
        self.norm_in(l, s, b, 1)
        hR = [hT.t(c) for c in range(KC)]
        PC = self.PC[l]
        g = self.big
        ffn_pending = None
        for c in range(FC):
            w = self.wtile(l, f"up{c}")
            wv = w.h[:, 0:2048].rearrange("p (q k n) -> p q k n", q=2, k=8)
            pa, pb = self.psn(), self.psn()
            for kc in range(8):
                self.mm(pa.h[:], wv[:, 0, kc, :], hT.h[:, kc, :], kc == 0, kc == 7, R=[w.t()] + hR, W=[pa.t()])
            for kc in range(8):
                self.mm(pb.h[:], wv[:, 1, kc, :], hT.h[:, kc, :], kc == 0, kc == 7, R=[w.t()] + hR, W=[pb.t()])
            ab = self.fp.get()
            self.V("pool", "tensor_copy", R=[self.halo_ff.t()], W=[ab.t()], out=ab.h[:, 0:2], in_=self.halo_ff.h[:, c, :])
            self.act(ab.h[:, 2:514], pa.h[:], AF.Copy, R=[pa.t()], W=[], PW=[ab.t()])
            acc = self.conv_taps(ab, 3, lambda k, c=c: PC.h[:, k * FC + c:k * FC + c + 1], [PC.t()], None, "dve")
            self.V("pool", "tensor_copy", R=[ab.t()], W=[], PW=[self.halo_ff.t()], out=self.halo_ff.h[:, c, :], in_=ab.h[:, 512:514])
            self.fp.free(ab)
            sl = self.fp.get()
            self.act(sl.h[:, 0:512], acc.h[:, 0:512], AF.Silu, R=[acc.t()], W=[sl.t()])
            self.psf(pa)
            self.fp.free(acc)
            if ffn_pending is not None:
                self.ffn_gate_mul(g, *ffn_pending)
            ffn_pending = (pb, sl, c)
        self.ffn_gate_mul(g, *ffn_pending)
        pss = []
        for j in range(8):
            if hoist and j == 4:
                self.norm_in(l, s, b + 1, 0)
            w = self.wtile(l, f"dn{j}")
            wv = w.h[:, 0:FC * 128].rearrange("p (k n) -> p k n", k=FC)
            p = self.psn()
            for kc in range(FC):
                self.mm(p.h[:], wv[:, kc, :], g.h[:, kc, :], kc == 0, kc == FC - 1, R=[w.t(), g.t(kc)], W=[p.t()])
            pss.append(p)
        self.post_norm_residual(l, s, b, 1, pss)


_CACHE = {}


def get_program(S, NSEQ):
    key = (S, NSEQ)
    if key not in _CACHE:
        kb = KB(S, NSEQ)
        _CACHE[key] = kb.build()
    return _CACHE[key]


def kernel(**inputs):
    x = np.ascontiguousarray(np.asarray(inputs["x"], dtype=np.float32))
    B, S, _ = x.shape
    ncores = 8 if B % 8 == 0 else (B if B < 8 else 1)
    NSEQ = B // ncores
    nc = get_program(S, NSEQ)
    c = np.ascontiguousarray(np.asarray(inputs["c"], dtype=np.float32))
    pos = np.ascontiguousarray(np.asarray(inputs["positions"], dtype=np.int32))
    wts = {n: np.ascontiguousarray(np.asarray(inputs[n], dtype=np.float32)) for n in WNAMES}
    in_maps = []
    for i in range(ncores):
        m = {"x": x[i * NSEQ:(i + 1) * NSEQ], "c": c[i * NSEQ:(i + 1) * NSEQ], "positions": pos[i * NSEQ:(i + 1) * NSEQ]}
        m.update(wts)
        in_maps.append(m)
    res = run_bass_kernel_spmd(nc, in_maps, core_ids=list(range(ncores)))
    out = np.concatenate([np.asarray(r["out"]) for r in res.results], axis=0)
    return out.astype(np.float32)
```
